# Optimizing a Trainium2 kernel written in Bass

```python
import jax, jax.numpy as jnp
from jax import lax
import numpy as np

D_MODEL = 1024
BATCH = 16
SEQ = 256
DEPTH = 2
DEC_BATCH = 4
DEC_SEQ = 4096
PAST_LEN = 256

GRID_W = 64
Q_BLOCK = 128
ROPE_THETA = 10000.0
RMS_EPS = 1e-6
NEG_INF = -1e30
MLA_HEADS = 8
MLA_Q_LORA = 256
MLA_KV_LORA = 256
MLA_NOPE_DIM = 64
MLA_ROPE_DIM = 32
MLA_V_DIM = 64
MLA_QK_DIM = MLA_NOPE_DIM + MLA_ROPE_DIM
MLA_SCALE = MLA_QK_DIM ** -0.5
NA_HEADS = 8
NA_HEAD_DIM = 64
NA_WIN_H = 8
NA_WIN_W = 16
NA_SCALE = NA_HEAD_DIM ** -0.5
W_IN_A = MLA_Q_LORA + MLA_KV_LORA + MLA_ROPE_DIM + 3 * NA_HEADS * NA_HEAD_DIM
W_OUT_A = MLA_HEADS * MLA_V_DIM + NA_HEADS * NA_HEAD_DIM
GQA_HEADS = 8
GQA_KV_HEADS = 2
GQA_HEAD_DIM = 128
GQA_SCALE = GQA_HEAD_DIM ** -0.5
W_IN_C = (GQA_HEADS + 2 * GQA_KV_HEADS) * GQA_HEAD_DIM
W_OUT_C = GQA_HEADS * GQA_HEAD_DIM
D_FF = -(-8 * D_MODEL // (3 * 256)) * 256
N_EVEN = (DEPTH + 1) // 2
N_ODD = DEPTH // 2

kernel_name = "hybrid_mla_natten_gqa_prefix_diffusion_step"


def rms_norm(x, g):
    xf = x.astype(jnp.float32)
    y = xf * lax.rsqrt(jnp.mean(xf * xf, axis=-1, keepdims=True) + RMS_EPS)
    return (y * g.astype(jnp.float32)).astype(x.dtype)


def ada_modulation(cond, w_mod, b_mod):
    m = jax.nn.silu(cond) @ w_mod + b_mod
    return jnp.split(m, 6, axis=-1)


def modulate(h, shift, scale):
    return h * (1.0 + scale) + shift


def axial_rope_tables(n_tokens, rot_dim):
    t = jnp.arange(n_tokens)
    row = (t // GRID_W).astype(jnp.float32)
    col = (t % GRID_W).astype(jnp.float32)
    axis_dim = rot_dim // 2
    inv_freq = ROPE_THETA ** (-jnp.arange(0, axis_dim, 2, dtype=jnp.float32) / axis_dim)
    ang = jnp.concatenate([row[:, None] * inv_freq, col[:, None] * inv_freq], axis=-1)
    return jnp.cos(ang), jnp.sin(ang)


def apply_rope(x, cos, sin):
    half = x.shape[-1] // 2
    xf = x.astype(jnp.float32)
    x1, x2 = xf[..., :half], xf[..., half:]
    c = cos[None, :, None, :]
    s = sin[None, :, None, :]
    return jnp.concatenate([x1 * c - x2 * s, x1 * s + x2 * c], axis=-1).astype(x.dtype)


def attend_blocked(q, k, v, scale):
    b, t, h, dq = q.shape
    hk, dv = k.shape[2], v.shape[-1]
    g = h // hk
    nb = t // Q_BLOCK
    qb = q.reshape(b, nb, Q_BLOCK, hk, g, dq).transpose(1, 0, 2, 3, 4, 5)

    def block(qi):
        s = jnp.einsum('bqkgd,bskd->bkgqs', qi, k).astype(jnp.float32) * scale
        p = jax.nn.softmax(s, axis=-1).astype(v.dtype)
        return jnp.einsum('bkgqs,bskd->bqkgd', p, v)

    o = lax.map(block, qb)
    return o.transpose(1, 0, 2, 3, 4, 5).reshape(b, t, h, dv)


def neighbourhood_attention(q, k, v, k_ctx, v_ctx, rpb):
    b, t, h, d = q.shape
    n_rows = t // GRID_W
    wh = min(NA_WIN_H, n_rows)
    qg = q.reshape(b, n_rows, GRID_W, h, d).transpose(1, 0, 2, 3, 4)
    kg = k.reshape(b, n_rows, GRID_W, h, d)
    vg = v.reshape(b, n_rows, GRID_W, h, d)
    rows = jnp.arange(n_rows)
    row_start = jnp.clip(rows - wh // 2, 0, n_rows - wh)
    cols = jnp.arange(GRID_W)
    col_start = jnp.clip(cols - NA_WIN_W // 2, 0, GRID_W - NA_WIN_W)
    col_mask = (cols[None, :] >= col_start[:, None]) & (cols[None, :] < col_start[:, None] + NA_WIN_W)
    dc_idx = jnp.clip(cols[None, :] - cols[:, None] + NA_WIN_W - 1, 0, 2 * NA_WIN_W - 2)

    def one_row(args):
        q_r, r, rs = args
        k_band = lax.dynamic_slice_in_dim(kg, rs, wh, axis=1)
        v_band = lax.dynamic_slice_in_dim(vg, rs, wh, axis=1)
        dr_idx = rs + jnp.arange(wh) - r + NA_WIN_H - 1
        bias = rpb[:, dr_idx][:, :, dc_idx]
        s_band = jnp.einsum('bqhd,bikhd->bhqik', q_r, k_band).astype(jnp.float32) * NA_SCALE
        s_band = s_band + bias.transpose(0, 2, 1, 3)[None].astype(jnp.float32)
        s_band = jnp.where(col_mask[:, None, :], s_band, NEG_INF)
        s_band = s_band.reshape(b, h, GRID_W, wh * GRID_W)
        s_ctx = jnp.einsum('bqhd,bchd->bhqc', q_r, k_ctx).astype(jnp.float32) * NA_SCALE
        p = jax.nn.softmax(jnp.concatenate([s_band, s_ctx], axis=-1), axis=-1).astype(v.dtype)
        p_band = p[..., :wh * GRID_W].reshape(b, h, GRID_W, wh, GRID_W)
        p_ctx = p[..., wh * GRID_W:]
        return (jnp.einsum('bhqik,bikhd->bqhd', p_band, v_band)
                + jnp.einsum('bhqc,bchd->bqhd', p_ctx, v_ctx))

    o = lax.map(one_row, (qg, rows, row_start))
    return o.transpose(1, 0, 2, 3, 4).reshape(b, t, h, d)


def even_project(h, w_in, q_norm, w_uq, kv_norm, w_ukv):
    b, L, _ = h.shape
    p = h @ w_in
    i0 = MLA_Q_LORA
    i1 = i0 + MLA_KV_LORA
    i2 = i1 + MLA_ROPE_DIM
    na_w = NA_HEADS * NA_HEAD_DIM
    cq, ckv, krope, nq, nk, nv = jnp.split(p, [i0, i1, i2, i2 + na_w, i2 + 2 * na_w], axis=-1)
    q = (rms_norm(cq, q_norm) @ w_uq).reshape(b, L, MLA_HEADS, MLA_QK_DIM)
    ckv = rms_norm(ckv, kv_norm)
    shp = (b, L, NA_HEADS, NA_HEAD_DIM)
    return q, ckv, krope, nq.reshape(shp), nk.reshape(shp), nv.reshape(shp)


def mla_expand(ckv, w_ukv, k_rope):
    b, L, _ = ckv.shape
    kv = (ckv @ w_ukv).reshape(b, L, MLA_HEADS, MLA_NOPE_DIM + MLA_V_DIM)
    k_nope, v = kv[..., :MLA_NOPE_DIM], kv[..., MLA_NOPE_DIM:]
    k_r = jnp.broadcast_to(k_rope[:, :, None, :], (b, L, MLA_HEADS, MLA_ROPE_DIM))
    return jnp.concatenate([k_nope, k_r], axis=-1), v


def even_mixer_context(h, w_in, q_norm, w_uq, kv_norm, w_ukv, w_out):
    b, L, _ = h.shape
    q, ckv, krope, nq, nk, nv = even_project(h, w_in, q_norm, w_uq, kv_norm, w_ukv)
    k, v = mla_expand(ckv, w_ukv, krope)
    a_mla = attend_blocked(q, k, v, MLA_SCALE)
    a_na = attend_blocked(nq, nk, nv, NA_SCALE)
    out = jnp.concatenate([a_mla.reshape(b, L, -1), a_na.reshape(b, L, -1)], axis=-1) @ w_out
    return out, ckv, krope, nk, nv


def even_mixer_latent(h, c_ckv, c_krope, c_nk, c_nv, w_in, q_norm, w_uq, kv_norm, w_ukv, rpb, w_out,
                      cos, sin):
    b, L, _ = h.shape
    q, ckv, krope, nq, nk, nv = even_project(h, w_in, q_norm, w_uq, kv_norm, w_ukv)
    q = jnp.concatenate([q[..., :MLA_NOPE_DIM], apply_rope(q[..., MLA_NOPE_DIM:], cos, sin)], axis=-1)
    krope = apply_rope(krope[:, :, None, :], cos, sin)[:, :, 0, :]
    k_lat, v_lat = mla_expand(ckv, w_ukv, krope)
    k_ctx, v_ctx = mla_expand(c_ckv, w_ukv, c_krope)
    a_mla = attend_blocked(q, jnp.concatenate([k_lat, k_ctx], axis=1),
                           jnp.concatenate([v_lat, v_ctx], axis=1), MLA_SCALE)
    a_na = neighbourhood_attention(nq, nk, nv, c_nk, c_nv, rpb)
    return jnp.concatenate([a_mla.reshape(b, L, -1), a_na.reshape(b, L, -1)], axis=-1) @ w_out


def odd_project(h, w_in, q_norm, k_norm):
    b, L, _ = h.shape
    p = h @ w_in
    q, k, v = jnp.split(p, [GQA_HEADS * GQA_HEAD_DIM, (GQA_HEADS + GQA_KV_HEADS) * GQA_HEAD_DIM], axis=-1)
    q = rms_norm(q.reshape(b, L, GQA_HEADS, GQA_HEAD_DIM), q_norm)
    k = rms_norm(k.reshape(b, L, GQA_KV_HEADS, GQA_HEAD_DIM), k_norm)
    v = v.reshape(b, L, GQA_KV_HEADS, GQA_HEAD_DIM)
    return q, k, v


def odd_mixer_context(h, w_in, q_norm, k_norm, w_out):
    b, L, _ = h.shape
    q, k, v = odd_project(h, w_in, q_norm, k_norm)
    out = attend_blocked(q, k, v, GQA_SCALE).reshape(b, L, -1) @ w_out
    return out, k, v


def odd_mixer_latent(h, c_k, c_v, w_in, q_norm, k_norm, w_out, cos, sin):
    b, L, _ = h.shape
    q, k, v = odd_project(h, w_in, q_norm, k_norm)
    q = apply_rope(q, cos, sin)
    k = apply_rope(k, cos, sin)
    o = attend_blocked(q, jnp.concatenate([k, c_k], axis=1), jnp.concatenate([v, c_v], axis=1), GQA_SCALE)
    return o.reshape(b, L, -1) @ w_out


def swiglu(h, w_in, w_out):
    gate, up = jnp.split(h @ w_in, 2, axis=-1)
    return (jax.nn.silu(gate) * up) @ w_out


def setup_inputs(seed: int = 0) -> dict:
    key = jax.random.key(seed)
    ks = iter(jax.random.split(key, 40))

    def nrm(shape, scale=1.0):
        return jax.random.normal(next(ks), shape, jnp.float32) * scale

    def gain(shape):
        return 1.0 + nrm(shape, 0.05)

    d = D_MODEL
    return {
        "x_prompt": nrm((BATCH, SEQ, d)),
        "x_sample": nrm((DEC_BATCH, DEC_SEQ, d)),
        "cache_mla_ckv": nrm((DEC_BATCH, N_EVEN, PAST_LEN, MLA_KV_LORA)),
        "cache_mla_krope": nrm((DEC_BATCH, N_EVEN, PAST_LEN, MLA_ROPE_DIM)),
        "cache_na_k": nrm((DEC_BATCH, N_EVEN, PAST_LEN, NA_HEADS, NA_HEAD_DIM)),
        "cache_na_v": nrm((DEC_BATCH, N_EVEN, PAST_LEN, NA_HEADS, NA_HEAD_DIM)),
        "cache_gqa_k": nrm((DEC_BATCH, N_ODD, PAST_LEN, GQA_KV_HEADS, GQA_HEAD_DIM)),
        "cache_gqa_v": nrm((DEC_BATCH, N_ODD, PAST_LEN, GQA_KV_HEADS, GQA_HEAD_DIM)),
        "c": nrm((DEC_BATCH, d)),
        "c_ctx": nrm((d,)),
        "w_mod": nrm((DEPTH, d, 6 * d), 0.5 * d ** -0.5),
        "b_mod": nrm((DEPTH, 6 * d), 0.02),
        "norm_mix": gain((DEPTH, d)),
        "norm_ffn": gain((DEPTH, d)),
        "norm_final": gain((d,)),
        "w_in_a": nrm((N_EVEN, d, W_IN_A), d ** -0.5),
        "mla_q_norm": gain((N_EVEN, MLA_Q_LORA)),
        "mla_w_uq": nrm((N_EVEN, MLA_Q_LORA, MLA_HEADS * MLA_QK_DIM), MLA_Q_LORA ** -0.5),
        "mla_kv_norm": gain((N_EVEN, MLA_KV_LORA)),
        "mla_w_ukv": nrm((N_EVEN, MLA_KV_LORA, MLA_HEADS * (MLA_NOPE_DIM + MLA_V_DIM)), MLA_KV_LORA ** -0.5),
        "na_rpb": nrm((N_EVEN, NA_HEADS, 2 * NA_WIN_H - 1, 2 * NA_WIN_W - 1), 0.5),
        "w_out_a": nrm((N_EVEN, W_OUT_A, d), W_OUT_A ** -0.5),
        "w_in_c": nrm((N_ODD, d, W_IN_C), d ** -0.5),
        "gqa_q_norm": gain((N_ODD, GQA_HEAD_DIM)),
        "gqa_k_norm": gain((N_ODD, GQA_HEAD_DIM)),
        "w_out_c": nrm((N_ODD, W_OUT_C, d), W_OUT_C ** -0.5),
        "w_ffn_in": nrm((DEPTH, d, 2 * D_FF), d ** -0.5),
        "w_ffn_out": nrm((DEPTH, D_FF, d), D_FF ** -0.5),
    }


def reference(x_prompt, x_sample, cache_mla_ckv, cache_mla_krope, cache_na_k, cache_na_v, cache_gqa_k,
              cache_gqa_v, c, c_ctx, w_mod, b_mod, norm_mix, norm_ffn, norm_final, w_in_a, mla_q_norm,
              mla_w_uq, mla_kv_norm, mla_w_ukv, na_rpb, w_out_a, w_in_c, gqa_q_norm, gqa_k_norm, w_out_c,
              w_ffn_in, w_ffn_out):
    xp = x_prompt
    xs = x_sample
    n_lat = x_sample.shape[1]
    cos_m, sin_m = axial_rope_tables(n_lat, MLA_ROPE_DIM)
    cos_g, sin_g = axial_rope_tables(n_lat, GQA_HEAD_DIM)
    cond_ctx = c_ctx[None, None, :]
    cond_lat = c[:, None, :]
    st_ckv, st_krope, st_nk, st_nv, st_gk, st_gv = [], [], [], [], [], []

    for l in range(DEPTH):
        sh1_p, sc1_p, g1_p, sh2_p, sc2_p, g2_p = ada_modulation(cond_ctx, w_mod[l], b_mod[l])
        sh1_s, sc1_s, g1_s, sh2_s, sc2_s, g2_s = ada_modulation(cond_lat, w_mod[l], b_mod[l])
        hp = modulate(rms_norm(xp, norm_mix[l]), sh1_p, sc1_p)
        hs = modulate(rms_norm(xs, norm_mix[l]), sh1_s, sc1_s)
        if l % 2 == 0:
            e = l // 2
            out_p, ckv, krope, nk, nv = even_mixer_context(
                hp, w_in_a[e], mla_q_norm[e], mla_w_uq[e], mla_kv_norm[e], mla_w_ukv[e], w_out_a[e])
            st_ckv.append(ckv)
            st_krope.append(krope)
            st_nk.append(nk)
            st_nv.append(nv)
            out_s = even_mixer_latent(
                hs, cache_mla_ckv[:, e], cache_mla_krope[:, e], cache_na_k[:, e], cache_na_v[:, e],
                w_in_a[e], mla_q_norm[e], mla_w_uq[e], mla_kv_norm[e], mla_w_ukv[e], na_rpb[e], w_out_a[e],
                cos_m, sin_m)
        else:
            o = l // 2
            out_p, gk, gv = odd_mixer_context(hp, w_in_c[o], gqa_q_norm[o], gqa_k_norm[o], w_out_c[o])
            st_gk.append(gk)
            st_gv.append(gv)
            out_s = odd_mixer_latent(hs, cache_gqa_k[:, o], cache_gqa_v[:, o], w_in_c[o], gqa_q_norm[o],
                                     gqa_k_norm[o], w_out_c[o], cos_g, sin_g)
        xp = xp + g1_p * out_p
        xs = xs + g1_s * out_s
        hp = modulate(rms_norm(xp, norm_ffn[l]), sh2_p, sc2_p)
        hs = modulate(rms_norm(xs, norm_ffn[l]), sh2_s, sc2_s)
        xp = xp + g2_p * swiglu(hp, w_ffn_in[l], w_ffn_out[l])
        xs = xs + g2_s * swiglu(hs, w_ffn_in[l], w_ffn_out[l])

    y_prompt = rms_norm(xp, norm_final)
    y_sample = rms_norm(xs, norm_final)
    state_mla_ckv = jnp.stack(st_ckv, axis=1)
    state_mla_krope = jnp.stack(st_krope, axis=1)
    state_na_k = jnp.stack(st_nk, axis=1)
    state_na_v = jnp.stack(st_nv, axis=1)
    state_gqa_k = jnp.stack(st_gk, axis=1)
    state_gqa_v = jnp.stack(st_gv, axis=1)
    return (y_prompt, y_sample, state_mla_ckv, state_mla_krope, state_na_k, state_na_v, state_gqa_k, state_gqa_v)
```

```python
from contextlib import ExitStack
import numpy as np
import concourse.bass as bass
import concourse.mybir as mybir
from concourse.bass_utils import run_bass_kernel_spmd

F32 = mybir.dt.float32
BF16 = mybir.dt.bfloat16
AF = mybir.ActivationFunctionType
ALU = mybir.AluOpType

D = 1024
TS = 2048
TP = 512
NT = TS + TP
NB = NT // 512
DFF = 2816
NFF = DFF // 128
EPS = 1e-6
MASKNEG = -200.0
RW = 160
RR = 23
R2 = RR + 1


class Buf:
    __slots__ = ("name", "w", "r", "pr", "semkey", "semcnt", "semkey2", "semcnt2", "excl")

    def __init__(self, name, excl=False):
        self.name = name
        self.excl = excl
        self.w = []
        self.r = []
        self.pr = []
        self.semkey = None
        self.semcnt = 0
        self.semkey2 = None
        self.semcnt2 = 0


class Stream:
    def __init__(self, name):
        self.name = name
        self.ops = []
        self.cnt = 0
        self.semkey = "eng_" + name
        self.seen = {}


class Prog:
    def __init__(self):
        self.streams = {n: Stream(n) for n in ("pe", "act", "dve", "pool", "sp")}
        self.semkeys = ["eng_pe", "eng_act", "eng_dve", "eng_pool"]
        self.nbuf = 0
        self.outs = []
        self.semcur = {}

    def buf(self, name=None, excl=False):
        self.nbuf += 1
        return Buf(f"{name or 'b'}{self.nbuf}", excl)

    def _need(self, st, ev):
        k, v = ev
        if st.name == "pe" and k == "eng_pe":
            return
        if k in self.semcur:
            v = max(v, self.semcur[k])
        if st.seen.get(k, 0) >= v:
            return
        st.seen[k] = v
        st.ops.append(("wait", k, v))

    def _sync(self, st, reads, writes, accum):
        reads = list(reads)
        writes = list(writes)
        accum = list(accum)
        r2 = [b for b in reads if not b.excl]
        w2 = writes + [b for b in reads if b.excl]
        for b in r2:
            for ev in b.w:
                self._need(st, ev)
        for b in w2:
            for ev in b.w:
                self._need(st, ev)
            for ev in b.r:
                self._need(st, ev)
        for b in accum:
            for ev in b.r:
                self._need(st, ev)
            if b.r:
                for ev in b.w:
                    self._need(st, ev)
            for ev in b.pr:
                self._need(st, ev)
        return r2, w2, accum

    def _record(self, ev, r2, w2, accum):
        for b in r2:
            b.r.append(ev)
        for b in w2:
            b.pr = b.w + b.r
            b.w = [ev]
            b.r = []
        for b in accum:
            if b.r:
                b.pr = b.w + b.r
                b.w = []
                b.r = []
            b.w.append(ev)

    def op(self, stname, meth, *args, reads=(), writes=(), accum=(), inc=True, **kw):
        st = self.streams[stname]
        inc = True
        r2, w2, ac = self._sync(st, reads, writes, accum)
        ev = (st.semkey, st.cnt + 1)
        fn = (meth, args, kw, False)
        if inc:
            st.cnt += 1
            st.ops.append(("op", fn, (st.semkey, 1)))
        else:
            st.ops.append(("op", fn, None))
        self._record(ev, r2, w2, ac)
        return ev

    def dma(self, qname, meth, *args, reads=(), writes=(), accum=(), sembuf=None, inc=16, nonc=False, **kw):
        st = self.streams[qname]
        r2, w2, ac = self._sync(st, reads, writes, accum)
        if qname == "pool":
            if sembuf.semkey2 is None:
                sembuf.semkey2 = f"s{len(self.semkeys)}_{sembuf.name}"
                self.semkeys.append(sembuf.semkey2)
            sembuf.semcnt2 += inc
            key, cnt = sembuf.semkey2, sembuf.semcnt2
        else:
            if sembuf.semkey is None:
                sembuf.semkey = f"d{len(self.semkeys)}_{sembuf.name}"
                self.semkeys.append(sembuf.semkey)
            sembuf.semcnt += inc
            key, cnt = sembuf.semkey, sembuf.semcnt
        self.semcur[key] = cnt
        ev = (key, cnt)
        st.ops.append(("op", (meth, args, kw, nonc), (key, inc)))
        self._record(ev, r2, w2, ac)
        return ev

    def barrier(self):
        for st in self.streams.values():
            for o in self.streams.values():
                if o.name != "sp" and o.cnt:
                    self._need(st, (o.semkey, o.cnt))
            for k, v in self.semcur.items():
                self._need(st, (k, v))

    def emit(self, nc, stack):
        st = self.streams["sp"]
        for ev in self.outs:
            self._need(st, ev)
        sems = {k: stack.enter_context(nc.semaphore(k)) for k in self.semkeys}
        block = stack.enter_context(nc.Block())

        def replay(s):
            def run(eng):
                for o in s.ops:
                    if o[0] == "wait":
                        eng.wait_ge(sems[o[1]], o[2])
                    else:
                        meth, args, kw, nonc = o[1]
                        if nonc:
                            with nc.allow_non_contiguous_dma(reason="strided"):
                                ins = getattr(eng, meth)(*args, **kw)
                        else:
                            ins = getattr(eng, meth)(*args, **kw)
                        if o[2] is not None:
                            ins.then_inc(sems[o[2][0]], o[2][1])
            return run

        for name, deco in (("pe", block.tensor), ("act", block.scalar), ("dve", block.vector),
                           ("pool", block.gpsimd), ("sp", block.sync)):
            if self.streams[name].ops:
                deco(replay(self.streams[name]))


def _vecmap():
    m = {}
    o = 0
    for nm, n in (("bmod0", 48), ("bmod1", 48), ("gmix0", 8), ("gmix1", 8), ("gffn0", 8), ("gffn1", 8),
                  ("gfin", 8), ("gq", 2), ("gkv", 2), ("ggq", 1), ("ggqs", 1), ("ggk", 1), ("ggks", 1),
                  ("cond", 16), ("eps", 1)):
        m[nm] = (o, n)
        o += n
    return m, o


VMAP, NVEC = _vecmap()

WA_CQ, WA_CKV, WA_KR, WA_KRS, WA_NQ, WA_NK, WA_NV, WA_N = 0, 256, 512, 544, 576, 1088, 1600, 2112
WC_Q, WC_K, WC_V, WC_QS, WC_KS, WC_N = 0, 1024, 1280, 1536, 2560, 2816


def build(debug_outs=()):
    nc = bass.Bass("TRN2", target_bir_lowering=False)
    P = Prog()

    def din(name, shape):
        return nc.dram_tensor(name, list(shape), F32, kind="ExternalInput").ap()

    def dout(name, shape):
        return nc.dram_tensor(name, list(shape), F32, kind="ExternalOutput").ap()

    def dscr(name, shape, dt=BF16):
        if name in debug_outs:
            return nc.dram_tensor(name, list(shape), dt, kind="ExternalOutput").ap()
        return nc.dram_tensor(name, list(shape), dt).ap()

    xs = din("xs", (TS, D))
    xp = din("xp", (TP, D))
    vec_d = din("vec", (128, NVEC))
    cst_d = din("cst", (128, 640))
    c_ckv = din("c_ckv", (256, 256))
    c_kr = din("c_kr", (256, 32))
    c_nk = din("c_nk", (256, 512))
    c_nv = din("c_nv", (256, 512))
    c_gk = din("c_gk", (256, 256))
    c_gv = din("c_gv", (256, 256))
    wmod = din("wmod", (2, D, 6 * D))
    wa = din("wa", (D, WA_N))
    wuq = din("wuq", (256, 1536))
    wukv = din("wukv", (256, 1024))
    woa = din("woa", (D, D))
    wc = din("wc", (D, WC_N))
    woc = din("woc", (D, D))
    wfi = din("wfi", (2, D, 2 * DFF))
    wfo = din("wfo", (2, DFF, D))
    ropem = din("ropem", (32, 2, TS))
    ropeg = din("ropeg", (128, 2, TS))
    rpbp = din("rpbp", (8, R2, RW))
    masks = din("masks", (18, 128, 512))
    y_s = dout("y_s", (TS, D))
    y_p = dout("y_p", (TP, D))
    st_ckv = dout("st_ckv", (TP, 256))
    st_kr = dout("st_kr", (TP, 32))
    st_nk = dout("st_nk", (TP, 512))
    st_nv = dout("st_nv", (TP, 512))
    st_gk = dout("st_gk", (TP, 256))
    st_gv = dout("st_gv", (TP, 256))
    QM = dscr("QM", (8, 96, NT))
    KVC = dscr("KVC", (288, TS))
    KVCa = dscr("KVCa", (576, TS))
    CKM = dscr("CKM", (288, 256))
    CKP = dscr("CKP", (288, TP))
    QN = dscr("QN", (8, 64, NT))
    KN = dscr("KN", (8, 64, TS))
    KNP = dscr("KNP", (8, 64, TP))
    KNS = dscr("KNS", (512, 256))
    KNSa = dscr("KNSa", (1024, 256))
    CKN = dscr("CKN", (8, 64, 256))
    VN = dscr("VN", (TS, 512))
    VNP = dscr("VNP", (TP, 512))
    VNS = dscr("VNS", (256, 512))
    VNSa = dscr("VNSa", (512, 512))
    QG = dscr("QG", (8, 128, NT))
    KG = dscr("KG", (256, TS))
    KGa = dscr("KGa", (512, TS))
    KGP = dscr("KGP", (2, 128, TP))
    CKG = dscr("CKG", (2, 128, 256))
    VG = dscr("VG", (TS, 256))
    VGa = dscr("VGa", (2 * TS, 256))
    VGP = dscr("VGP", (TP, 256))
    PAIRS = [[0, 1], [2, 3], [4, 5], [6, 7]]

    with ExitStack() as stack:
        def sb(name, shape, dt):
            return stack.enter_context(nc.sbuf_tensor(name, list(shape), dt))

        xT = sb("xT", (128, 8, NT), F32)
        hT = sb("hT", (128, 8, NT), BF16)
        bx = [[P.buf("x") for _ in range(NB)] for _ in range(8)]
        bh = [[P.buf("h") for _ in range(NB)] for _ in range(8)]
        cst = sb("cst_sb", (128, 640), F32)
        b_cst = P.buf("cst")
        ident = cst[:, 0:128]
        sel = [cst[:, 128:256], cst[:, 256:384]]
        onesb = sb("onesb", (128, 128), BF16)
        b_onesb = P.buf("onesb")
        identb = sb("identb", (128, 128), BF16)
        b_identb = P.buf("identb")
        J8b = sb("J8b", (128, 128), BF16)
        b_J8b = P.buf("J8b")
        vec = sb("vec_sb", (128, NVEC), F32)
        b_vec = P.buf("vec")
        silc = sb("silc", (128, 16), F32)
        b_silc = P.buf("silc")
        modv = sb("modv", (128, 2, 48), F32)
        b_modv = P.buf("modv")
        dm = sb("dm", (128, 2, 6, 8), F32)
        b_dm = P.buf("dm")
        rec = [sb("rec0", (128, 512), F32), sb("rec1", (128, 512), F32)]
        b_rec = [P.buf("rec0"), P.buf("rec1")]
        ps = [stack.enter_context(nc.psum_tensor(f"ps{i}", [128, 512], F32)) for i in range(8)]
        bps = [P.buf(f"ps{i}", excl=True) for i in range(8)]
        rr = {"ps": 0}

        def nbank():
            i = rr["ps"]
            rr["ps"] = (i + 1) % 8
            if i == rr.get("skip"):
                return nbank()
            return i

        def V(nm, i=0, n=1):
            o, _ = VMAP[nm]
            return vec[:, o + i:o + i + n]

        uniq = {"n": 0}

        class Pool_:
            def __init__(self, name, shape, dt, n, stk=None):
                stk = stack if stk is None else stk
                uniq["n"] += 1
                self.t = [stk.enter_context(nc.sbuf_tensor(f"{name}{i}_{uniq['n']}", list(shape), dt)) for i in range(n)]
                self.b = [P.buf(f"{name}{i}") for i in range(n)]
                self.i = 0

            def get(self):
                i = self.i
                self.i = (i + 1) % len(self.t)
                return self.t[i], self.b[i]

        f32t = Pool_("f32t", (128, 512), F32, 5)
        rsd = Pool_("rsd", (128, 512), F32, 2)
        bft = Pool_("bft", (128, 512), BF16, 4)
        pl = {}

        def open_pools(stk):
            pl["w32"] = Pool_("w32", (128, 2048), F32, 2, stk)
            pl["wbf"] = Pool_("wbf", (128, 2048), BF16, 2, stk)
            pl["stg"] = Pool_("stg", (128, 1024), F32, 2, stk)

        class _PL:
            def __init__(self, k):
                self.k = k

            def get(self):
                return pl[self.k].get()

        w32, wbf, stg = _PL("w32"), _PL("wbf"), _PL("stg")
        scopeA = ExitStack()
        open_pools(scopeA)

        def dma_in(dst_ap, dst_b, src_ap, q="sp", reads=(), accum=False):
            if accum:
                return P.dma(q, "dma_start", out=dst_ap, in_=src_ap, reads=reads, accum=[dst_b], sembuf=dst_b)
            return P.dma(q, "dma_start", out=dst_ap, in_=src_ap, reads=reads, writes=[dst_b], sembuf=dst_b)

        def dma_nc(dst_ap, dst_b, src_ap, q="sp", reads=(), accum=False):
            if accum:
                return P.dma(q, "dma_start", out=dst_ap, in_=src_ap, reads=reads, accum=[dst_b], sembuf=dst_b, nonc=True)
            return P.dma(q, "dma_start", out=dst_ap, in_=src_ap, reads=reads, writes=[dst_b], sembuf=dst_b, nonc=True)

        def dma_out(dst_ap, dst_b, src_ap, src_b, q="sp", final=False, nonc=False):
            ev = P.dma(q, "dma_start", out=dst_ap, in_=src_ap, reads=[src_b], accum=[dst_b], sembuf=src_b, nonc=nonc)
            if final:
                P.outs.append(ev)
            return ev

        def dbg_dump(name, src_ap, shape, reads):
            if name not in debug_outs:
                return
            t = nc.dram_tensor(name, list(shape), BF16, kind="ExternalOutput").ap()
            db = P.buf(name)
            ev = P.dma("sp", "dma_start", out=t, in_=src_ap, reads=reads, writes=[db], sembuf=db)
            P.outs.append(ev)
            P.barrier()

        def load_w(src_ap, nk, ncols):
            t32, b32 = w32.get()
            tb_, bb = wbf.get()
            v32 = t32[:, 0:nk * ncols].rearrange("p (k n) -> p k n", k=nk)
            vb = tb_[:, 0:nk * ncols].rearrange("p (k n) -> p k n", k=nk)
            dma_in(v32, b32, src_ap)
            P.op("pool", "tensor_copy", vb, v32, reads=[b32], writes=[bb])
            return vb, bb

        def wview(w_ap, c0, ncols, nk=8, r0=0):
            return w_ap[r0:r0 + nk * 128, :].rearrange("(k p) n -> p k n", p=128)[:, :, c0:c0 + ncols]

        def mm_acc(bank, m0, M, n, pairs, reads):
            last = len(pairs) - 1
            for i, (l, r) in enumerate(pairs):
                P.op("pe", "matmul", ps[bank][0:M, 0:n], l, r, start=(i == 0), stop=(i == last),
                     reads=reads, writes=[bps[bank]], inc=(i == last))

        dma_in(cst[:], b_cst, cst_d[:, :])
        dma_in(vec[:], b_vec, vec_d[:, :])
        P.op("act", "copy", onesb[:], cst[:, 384:512], reads=[b_cst], writes=[b_onesb])
        P.op("act", "copy", identb[:], cst[:, 0:128], reads=[b_cst], writes=[b_identb])
        P.op("act", "mul", J8b[:], cst[:, 512:640], 8.0, reads=[b_cst], writes=[b_J8b])
        P.op("act", "activation", silc[:], V("cond", 0, 16), AF.Silu, reads=[b_vec], writes=[b_silc])
        for m in range(2):
            P.op("pool", "memset", rec[m][:], 0.0, writes=[b_rec[m]])

        def load_x(src, ntiles, tok0, only=None):
            for t in range(ntiles):
                if only is not None and t != only:
                    continue
                s_t, s_b = stg.get()
                dma_in(s_t[:], s_b, src[t * 128:(t + 1) * 128, :])
                tok = tok0 + t * 128
                tb = tok // 512
                for half in range(2):
                    bk = nbank()
                    for c4 in range(4):
                        c = half * 4 + c4
                        P.op("pe", "transpose", ps[bk][:, c4 * 128:(c4 + 1) * 128], s_t[:, c * 128:(c + 1) * 128], ident,
                             reads=[s_b, b_cst], writes=[bps[bk]], inc=(c4 == 3))
                    eng = "dve" if half == 0 else "act"
                    dst = xT[:, half * 4:half * 4 + 4, tok:tok + 128]
                    srcv = ps[bk][:, 0:512].rearrange("p (c n) -> p c n", c=4)
                    if eng == "dve":
                        P.op("dve", "tensor_copy", dst, srcv, reads=[bps[bk]],
                             accum=[bx[half * 4 + i][tb] for i in range(4)])
                    else:
                        P.op("act", "copy", dst, srcv, reads=[bps[bk]],
                             accum=[bx[half * 4 + i][tb] for i in range(4)])

        XJOBS = [(xs, TS // 128, 0, t) for t in range(TS // 128)] + [(xp, TP // 128, TS, t) for t in range(TP // 128)]

        def x_hook(t):
            if XJOBS:
                src_, nt_, tok0_, tt_ = XJOBS.pop(0)
                load_x(src_, nt_, tok0_, only=tt_)

        def tr_store(src_t, src_b, col0, ncol, dst_ap, dst_b):
            bk = nbank()
            P.op("pe", "transpose", ps[bk][0:ncol, 0:128], src_t[:, col0:col0 + ncol], ident,
                 reads=[src_b, b_cst], writes=[bps[bk]])
            t_, b_ = bft.get()
            P.op("dve", "tensor_copy", t_[0:ncol, 0:128], ps[bk][0:ncol, 0:128], reads=[bps[bk]], writes=[b_])
            dma_out(dst_ap, dst_b, t_[0:ncol, 0:128], b_)

        b_CKM, b_CKN, b_CKG = P.buf("CKM"), P.buf("CKN"), P.buf("CKG")
        for tt in range(2):
            s_t, s_b = stg.get()
            dma_in(s_t[:, 0:256], s_b, c_ckv[tt * 128:(tt + 1) * 128, :])
            dma_in(s_t[:, 256:288], s_b, c_kr[tt * 128:(tt + 1) * 128, :], accum=True)
            for k in range(2):
                tr_store(s_t, s_b, k * 128, 128, CKM[k * 128:(k + 1) * 128, tt * 128:(tt + 1) * 128], b_CKM)
            tr_store(s_t, s_b, 256, 32, CKM[256:288, tt * 128:(tt + 1) * 128], b_CKM)
            s_t, s_b = stg.get()
            dma_in(s_t[:, 0:512], s_b, c_nk[tt * 128:(tt + 1) * 128, :])
            dma_in(s_t[:, 512:768], s_b, c_gk[tt * 128:(tt + 1) * 128, :], accum=True)
            for h in range(8):
                tr_store(s_t, s_b, h * 64, 64, CKN[h, :, tt * 128:(tt + 1) * 128], b_CKN)
            for g in range(2):
                tr_store(s_t, s_b, 512 + g * 128, 128, CKG[g, :, tt * 128:(tt + 1) * 128], b_CKG)

        def compute_mod(l, hook=None):
            bk = nbank()
            rr["skip"] = bk
            for t in range(24):
                if hook is not None:
                    hook(t)
                wv32, wb32 = w32.get()
                v32 = wv32[:, 0:2048].rearrange("p (k n) -> p k n", k=8)
                dma_in(v32, wb32, wmod[l].rearrange("(k p) n -> p k n", p=128)[:, :, t * 256:(t + 1) * 256])
                for cc in range(2):
                    ch = t * 2 + cc
                    for k in range(8):
                        P.op("pe", "matmul",
                            ps[bk][:, ch * 2:ch * 2 + 2], v32[:, k, cc * 128:(cc + 1) * 128],
                            silc[:, k * 2:k * 2 + 2], start=(k == 0), stop=(k == 7),
                            reads=[wb32, b_silc], writes=[bps[bk]], inc=(k == 7))
            rr["skip"] = None
            pv = ps[bk][:, 0:96].rearrange("p (c two) -> p c two", two=2)
            bo, _ = VMAP[f"bmod{l}"]
            for cd in range(2):
                P.op("dve", "tensor_tensor", modv[:, cd, :], pv[:, :, cd], vec[:, bo:bo + 48], ALU.add,
                     reads=[bps[bk], b_vec], accum=[b_modv] if cd else (), writes=[] if cd else [b_modv])
            gm, gf = V(f"gmix{l}", 0, 8), V(f"gffn{l}", 0, 8)
            for cd in range(2):
                def s_(i, cd=cd):
                    return modv[:, cd, i * 8:(i + 1) * 8]
                first = (cd == 0)
                ops = [(0, s_(1), gm, True), (1, s_(0), None, False), (2, s_(2), None, False),
                       (3, s_(4), gf, True), (4, s_(3), None, False), (5, s_(5), None, False)]
                for j, (slot, src, g, isA) in enumerate(ops):
                    kw = dict(reads=[b_modv, b_vec])
                    if first and j == 0:
                        kw["writes"] = [b_dm]
                    else:
                        kw["accum"] = [b_dm]
                    if isA:
                        P.op("dve", "scalar_tensor_tensor",
                            dm[:, cd, slot, :], src, 1.0, g, ALU.add, ALU.mult, **kw)
                    else:
                        P.op("dve", "tensor_copy", dm[:, cd, slot, :], src, **kw)

        def rstd_block(src_fn, nchunks, tb, n, scale, reads_fn):
            bk = nbank()
            for c in range(nchunks):
                q_t, q_b = bft.get()
                P.op("act", "activation", q_t[:, 0:n], src_fn(c), AF.Square, reads=reads_fn(c), writes=[q_b])
                P.op("pe", "matmul", ps[bk][:, 0:n], onesb[:], q_t[:, 0:n], start=(c == 0), stop=(c == nchunks - 1),
                     reads=[q_b, b_onesb], writes=[bps[bk]], inc=(c == nchunks - 1))
            r_t, r_b = rsd.get()
            P.op("act", "activation", r_t[:, 0:n], ps[bk][:, 0:n], AF.Sqrt, bias=V("eps", 0, 1), scale=scale, reads=[bps[bk], b_vec], writes=[r_b])
            P.op("dve", "reciprocal", r_t[:, 0:n], r_t[:, 0:n], reads=[], writes=[r_b])
            return r_t, r_b

        def norm_mod(slotA, slotB):
            for tb in range(NB):
                cd = 0 if tb < 4 else 1
                cols = slice(tb * 512, (tb + 1) * 512)
                r_t, r_b = rstd_block(lambda c: xT[:, c, cols], 8, tb, 512, 1.0 / D, lambda c: [bx[c][tb]])
                for c in range(8):
                    t_t, t_b = f32t.get()
                    P.op("dve", "tensor_tensor", t_t[:], xT[:, c, cols], r_t[:], ALU.mult,
                         reads=[bx[c][tb], r_b], writes=[t_b])
                    P.op("act", "activation", hT[:, c, cols], t_t[:], AF.Identity,
                                                                      bias=dm[:, cd, slotB, c:c + 1], scale=dm[:, cd, slotA, c:c + 1],
                         reads=[t_b, b_dm], writes=[bh[c][tb]])

        def proj_h(wt, wb_, c0, M, tb, bank, src=None, srcb=None, nk=8):
            src = hT if src is None else src
            cols = slice(tb * 512, (tb + 1) * 512)
            pairs = [(wt[:, k, c0:c0 + M], src[:, k, cols]) for k in range(nk)]
            rb = [wb_] + ([bh[k][tb] for k in range(nk)] if srcb is None else srcb)
            mm_acc(bank, 0, M, 512, pairs, rb)

        def resid(bank, c, tb, cd, slotG):
            cols = slice(tb * 512, (tb + 1) * 512)
            P.op("dve", "scalar_tensor_tensor", xT[:, c, cols], ps[bank][:, 0:512], dm[:, cd, slotG, c:c + 1],
                                                        xT[:, c, cols], ALU.mult, ALU.add,
                 reads=[bps[bank], b_dm], writes=[bx[c][tb]])

        def out_proj(w_ap, slotG):
            for t in range(4):
                wt, wb_ = load_w(wview(w_ap, t * 256, 256), 8, 256)
                for cc in range(2):
                    c = t * 2 + cc
                    for tb in range(NB):
                        bk = nbank()
                        proj_h(wt, wb_, cc * 128, 128, tb, bk)
                        resid(bk, c, tb, 0 if tb < 4 else 1, slotG)

        def ffn(l):
            GRP = 4
            with ExitStack() as s2:
                hid = s2.enter_context(nc.sbuf_tensor(f"hid{l}", [128, GRP, NT], BF16))
                bhid = [[P.buf("hid") for _ in range(NB)] for _ in range(GRP)]
                for g0 in range(0, NFF, GRP):
                    gn = min(GRP, NFF - g0)
                    for jl in range(gn):
                        j = g0 + jl
                        t32, b32 = w32.get()
                        tb_, bb = wbf.get()
                        v32 = t32[:, 0:2048].rearrange("p (k n) -> p k n", k=8)
                        vb = tb_[:, 0:2048].rearrange("p (k n) -> p k n", k=8)
                        wsrc = wfi[l].rearrange("(k p) n -> p k n", p=128)
                        dma_in(v32[:, :, 0:128], b32, wsrc[:, :, j * 128:(j + 1) * 128])
                        dma_in(v32[:, :, 128:256], b32, wsrc[:, :, DFF + j * 128:DFF + (j + 1) * 128], accum=True)
                        P.op("pool", "tensor_copy", vb, v32, reads=[b32], writes=[bb])
                        for tb in range(NB):
                            bg, bu = nbank(), nbank()
                            proj_h(vb, bb, 0, 128, tb, bg)
                            proj_h(vb, bb, 128, 128, tb, bu)
                            s_t, s_b = f32t.get()
                            P.op("act", "activation", s_t[:], ps[bg][:, 0:512], AF.Silu, reads=[bps[bg]], writes=[s_b])
                            P.op("dve", "tensor_tensor",
                                hid[:, jl, tb * 512:(tb + 1) * 512], s_t[:], ps[bu][:, 0:512], ALU.mult,
                                reads=[s_b, bps[bu]], writes=[bhid[jl][tb]])
                    for t in range(4):
                        src = wfo[l][g0 * 128:(g0 + gn) * 128, :].rearrange("(k p) n -> p k n", p=128)[:, :, t * 256:(t + 1) * 256]
                        wt, wb_ = load_w(src, gn, 256)
                        for cc in range(2):
                            c = t * 2 + cc
                            for tb in range(NB):
                                bk = nbank()
                                proj_h(wt, wb_, cc * 128, 128, tb, bk, src=hid, srcb=[bhid[k][tb] for k in range(gn)], nk=gn)
                                resid(bk, c, tb, 0 if tb < 4 else 1, 5)
                flush_fin(); P.barrier()

        SB = (0, 1, 2)
        OB = (3, 4)
        UB = (5, 6)
        BB = 7
        actr = {"i": 0}
        pend = {"fin": None}

        def flush_fin():
            f_ = pend["fin"]
            pend["fin"] = None
            if f_ is not None:
                f_()

        def attention(q_ap, q_b, nq, dk, kchunks, vchunks, scale, mode, out_ap, out_bufs, echunks=None):
            it = actr["i"]
            actr["i"] += 1
            ob = OB[it % 2]
            ub = UB[it % 2]
            n = len(kchunks)
            ptile = [None] * n

            def issue_s(i):
                bk = SB[i % 3]
                kap, kb = kchunks[i]
                he = echunks is not None and echunks[i] is not None
                P.op("pe", "matmul", ps[bk][:, 0:nq], kap, q_ap, start=True, stop=(not he),
                     reads=[kb, q_b], writes=[bps[bk]])
                if he:
                    bap, bb_, map_, mb_ = echunks[i]
                    P.op("pe", "matmul", ps[bk][:, 0:nq], identb[:], map_, start=False, stop=False,
                         reads=[mb_, b_identb], writes=[bps[bk]])
                    P.op("pe", "matmul", ps[bk][:, 0:nq], J8b[:], bap, start=False, stop=True,
                         reads=[bb_, b_J8b], writes=[bps[bk]])

            issue_s(0)
            if n > 1:
                issue_s(1)
            flush_fin()
            for i in range(n):
                bk = SB[i % 3]
                p_t, p_b = bft.get()
                P.op("act", "activation", p_t[:, 0:nq], ps[bk][:, 0:nq], AF.Exp, scale=scale,
                     reads=[bps[bk]], writes=[p_b])
                if i + 2 < n:
                    issue_s(i + 2)
                vap, vb = vchunks[i]
                P.op("pe", "matmul", ps[ob][:, 0:nq], vap, p_t[:, 0:nq], start=(i == 0), stop=(i == n - 1),
                     reads=[vb, p_b], writes=[bps[ob]], inc=(mode != 2 and i == n - 1))
                if mode == 2:
                    P.op("pe", "matmul", ps[ub][:, 0:nq], onesb[:], p_t[:, 0:nq], start=(i == 0), stop=(i == n - 1),
                         reads=[p_b, b_onesb], writes=[bps[ub]], inc=(i == n - 1))
            pend["fin"] = lambda: fin_(mode, nq, ob, ub, out_ap, out_bufs)

        def fin_(mode, nq, ob, ub, out_ap, out_bufs):
            if mode == 2:
                r_t, r_b = f32t.get()
                P.op("dve", "reciprocal", r_t[:, 0:nq], ps[ub][:, 0:nq], reads=[bps[ub]], writes=[r_b])
                P.op("dve", "tensor_tensor", out_ap, ps[ob][:, 0:nq], r_t[:, 0:nq], ALU.mult,
                     reads=[bps[ob], r_b], accum=out_bufs)
            else:
                ro = slice(0, 64) if mode == 0 else slice(64, 128)
                rs = slice(64, 128) if mode == 0 else slice(0, 64)
                P.op("dve", "reciprocal", rec[mode][rs, 0:nq], ps[ob][rs, 0:nq], reads=[bps[ob]], writes=[b_rec[mode]])
                P.op("pe", "matmul", ps[BB][:, 0:nq], sel[mode], rec[mode][:, 0:nq], start=True, stop=True,
                     reads=[b_rec[mode], b_cst], writes=[bps[BB]])
                o_t, o_b = f32t.get()
                P.op("dve", "tensor_copy", o_t[ro, 0:nq], ps[ob][ro, 0:nq], reads=[bps[ob]], writes=[o_b])
                P.op("dve", "tensor_tensor", out_ap, o_t[ro, 0:nq], ps[BB][ro, 0:nq], ALU.mult,
                     reads=[o_b, bps[BB]], accum=out_bufs)

        def store_tok_major(src_t, src_b, rows, ntile, dst_fn, dst_b):
            for t in range(ntile):
                bk = nbank()
                P.op("pe", "transpose", ps[bk][:, 0:rows], src_t[0:rows, t * 128:(t + 1) * 128], ident[0:rows, 0:rows],
                     reads=[src_b, b_cst], writes=[bps[bk]])
                o_t, o_b = f32t.get()
                P.op("dve", "tensor_copy", o_t[:, 0:rows], ps[bk][:, 0:rows], reads=[bps[bk]], writes=[o_b])
                dma_out(dst_fn(t), dst_b, o_t[:, 0:rows], o_b, final=True, nonc=True)

        def layer0():
            compute_mod(0, hook=x_hook)
            while XJOBS:
                x_hook(0)
            norm_mod(0, 1)
            b = {n: P.buf(n) for n in ("QM", "KVC", "KVCa", "CKP", "QN", "KN", "KNP", "KNS", "KNSa", "VN", "VNP", "VNS", "VNSa", "st")}
            with ExitStack() as s2:
                c32 = Pool_("c32", (128, 512), F32, 4, s2)
                nT = Pool_("nT", (128, 512), BF16, 4, s2)
                rp = s2.enter_context(nc.sbuf_tensor("rp", [128, 2, 512], F32))
                b_rp = P.buf("rp")
                wq_t = s2.enter_context(nc.sbuf_tensor("wq_t", [128, 2, 1536], BF16))
                b_wq = P.buf("wq")
                wts = [load_w(wview(wa, WA_CQ, 256), 8, 256), load_w(wview(wa, WA_CKV, 256), 8, 256)]
                P.dma("pool", "dma_start", out=wq_t[:], in_=wuq.rearrange("(k p) n -> p k n", p=128),
                      writes=[b_wq], sembuf=b_wq)
                for tb in range(NB):
                    cols = slice(tb * 512, (tb + 1) * 512)
                    for which in range(2):
                        wt, wb_ = wts[which]
                        gname = "gq" if which == 0 else "gkv"
                        n32 = [None, None]
                        n32b = [None, None]
                        for cc in range(2):
                            bk = nbank()
                            proj_h(wt, wb_, cc * 128, 128, tb, bk)
                            c_t, c_b = c32.get()
                            P.op("act", "copy", c_t[:], ps[bk][:, 0:512], reads=[bps[bk]], writes=[c_b])
                            n32[cc], n32b[cc] = c_t, c_b
                        r_t, r_b = rstd_block(lambda c: n32[c][:], 2, tb, 512, 1.0 / 256, lambda c: [n32b[c]])
                        n_t = [None, None]
                        n_b = [None, None]
                        for c in range(2):
                            P.op("dve", "scalar_tensor_tensor", n32[c][:], n32[c][:], V(gname, c, 1), r_t[:], ALU.mult, ALU.mult,
                                 reads=[r_b, b_vec], writes=[n32b[c]])
                            nb_t, nb_b = nT.get()
                            P.op("act", "copy", nb_t[:], n32[c][:], reads=[n32b[c]], writes=[nb_b])
                            n_t[c], n_b[c] = nb_t, nb_b
                        if which == 1:
                            for c in range(2):
                                if tb < 4:
                                    dma_out(KVC[c * 128:(c + 1) * 128, cols], b["KVC"], n_t[c][:], n_b[c])
                                else:
                                    dma_out(CKP[c * 128:(c + 1) * 128, :], b["CKP"], n_t[c][:], n_b[c])
                                    store_tok_major(n32[c], n32b[c], 128, 4,
                                                    lambda t, c=c: st_ckv[t * 128:(t + 1) * 128, c * 128:(c + 1) * 128], b["st"])
                        else:
                            if tb < 4:
                                dma_in(rp[64:96, :, :], b_rp, ropem[:, :, cols])
                            for h in range(8):
                                b1 = nbank()
                                mm_acc(b1, 0, 96, 512, [(wq_t[:, k, h * 96:(h + 1) * 96], n_t[k][:]) for k in range(2)], [b_wq, n_b[0], n_b[1]])
                                q_t, q_b = bft.get()
                                P.op("act", "copy", q_t[0:64, :], ps[b1][0:64, 0:512], reads=[bps[b1]], writes=[q_b])
                                if tb < 4:
                                    b2 = nbank()
                                    mm_acc(b2, 0, 96, 512, [(wq_t[:, k, 768 + h * 96:768 + (h + 1) * 96], n_t[k][:]) for k in range(2)], [b_wq, n_b[0], n_b[1]])
                                    u_t, u_b = f32t.get()
                                    v_t, v_b = f32t.get()
                                    P.op("dve", "tensor_tensor", u_t[64:96, :], ps[b1][64:96, 0:512], rp[64:96, 0, :], ALU.mult,
                                         reads=[bps[b1], b_rp], writes=[u_b])
                                    P.op("dve", "tensor_tensor", v_t[64:96, :], ps[b2][64:96, 0:512], rp[64:96, 1, :], ALU.mult,
                                         reads=[bps[b2], b_rp], writes=[v_b])
                                    P.op("dve", "tensor_tensor", q_t[64:96, :], u_t[64:96, :], v_t[64:96, :], ALU.add,
                                         reads=[u_b, v_b], accum=[q_b])
                                else:
                                    P.op("dve", "tensor_copy", q_t[64:96, :], ps[b1][64:96, 0:512], reads=[bps[b1]], accum=[q_b])
                                dma_out(QM[h, :, cols], b["QM"], q_t[0:96, :], q_b)
                wt, wb_ = load_w(wview(wa, WA_KR, 64), 8, 64)
                for tb in range(NB):
                    cols = slice(tb * 512, (tb + 1) * 512)
                    b1 = nbank()
                    proj_h(wt, wb_, 0, 32, tb, b1)
                    k_t, k_b = bft.get()
                    if tb < 4:
                        b2 = nbank()
                        proj_h(wt, wb_, 32, 32, tb, b2)
                        dma_in(rp[0:32, :, :], b_rp, ropem[:, :, cols])
                        u_t, u_b = f32t.get()
                        v_t, v_b = f32t.get()
                        P.op("dve", "tensor_tensor", u_t[0:32, :], ps[b1][0:32, 0:512], rp[0:32, 0, :], ALU.mult,
                             reads=[bps[b1], b_rp], writes=[u_b])
                        P.op("dve", "tensor_tensor", v_t[0:32, :], ps[b2][0:32, 0:512], rp[0:32, 1, :], ALU.mult,
                             reads=[bps[b2], b_rp], writes=[v_b])
                        P.op("dve", "tensor_tensor", k_t[0:32, :], u_t[0:32, :], v_t[0:32, :], ALU.add,
                             reads=[u_b, v_b], writes=[k_b])
                        dma_out(KVC[256:288, cols], b["KVC"], k_t[0:32, :], k_b)
                    else:
                        u_t, u_b = f32t.get()
                        P.op("act", "copy", u_t[0:32, :], ps[b1][0:32, 0:512], reads=[bps[b1]], writes=[u_b])
                        P.op("dve", "tensor_copy", k_t[0:32, :], u_t[0:32, :], reads=[u_b], writes=[k_b])
                        dma_out(CKP[256:288, :], b["CKP"], k_t[0:32, :], k_b)
                        store_tok_major(u_t, u_b, 32, 4, lambda t: st_kr[t * 128:(t + 1) * 128, :], b["st"])
                flush_fin(); P.barrier()
            P.dma("pool", "collective_compute", "AllGather", ALU.bypass, replica_groups=PAIRS,
                                                         ins=[KVC[:, :].opt()], outs=[KVCa[:, :].opt()],
                  reads=[b["KVC"]], writes=[b["KVCa"]], sembuf=b["KVCa"], inc=1)
            for which in range(2):
                for t in range(2):
                    wt, wb_ = load_w(wview(wa, (WA_NQ if which == 0 else WA_NK) + t * 256, 256), 8, 256)
                    for cc in range(2):
                        hp = t * 2 + cc
                        for tb in range(NB):
                            cols = slice(tb * 512, (tb + 1) * 512)
                            bk = nbank()
                            proj_h(wt, wb_, cc * 128, 128, tb, bk)
                            o_t, o_b = bft.get()
                            if which == 1 and tb == 4:
                                f_t, f_b = f32t.get()
                                P.op("act", "copy", f_t[:], ps[bk][:, 0:512], reads=[bps[bk]], writes=[f_b])
                                P.op("dve", "tensor_copy", o_t[:], f_t[:], reads=[f_b], writes=[o_b])
                                store_tok_major(f_t, f_b, 128, 4, lambda t_, hp=hp: st_nk[t_ * 128:(t_ + 1) * 128, hp * 128:(hp + 1) * 128], b["st"])
                            else:
                                P.op("act", "copy", o_t[:], ps[bk][:, 0:512], reads=[bps[bk]], writes=[o_b])
                            for hh in range(2):
                                h = hp * 2 + hh
                                rows = slice(hh * 64, hh * 64 + 64)
                                if which == 0:
                                    dma_out(QN[h, :, cols], b["QN"], o_t[rows, :], o_b)
                                elif tb < 4:
                                    dma_out(KN[h, :, cols], b["KN"], o_t[rows, :], o_b)
                                    if tb == 3:
                                        for rr_ in range(4):
                                            dma_out(KNS[h * 64:(h + 1) * 64, rr_ * 64:(rr_ + 1) * 64], b["KNS"],
                                                    o_t[rows, (7 - rr_) * 64:(8 - rr_) * 64], o_b)
                                else:
                                    dma_out(KNP[h, :, :], b["KNP"], o_t[rows, :], o_b)
            for t in range(2):
                wt, wb_ = load_w(wview(wa, WA_NV + t * 256, 256), 8, 256)
                for tt in range(NT // 128):
                    bk = nbank()
                    tb = tt // 4
                    mm_acc(bk, 0, 128, 256, [(hT[:, k, tt * 128:(tt + 1) * 128], wt[:, k, :]) for k in range(8)],
                           [wb_] + [bh[k][tb] for k in range(8)])
                    o_t, o_b = bft.get()
                    if tb == 4:
                        f_t, f_b = f32t.get()
                        P.op("act", "copy", f_t[:, 0:256], ps[bk][:, 0:256], reads=[bps[bk]], writes=[f_b])
                        P.op("dve", "tensor_copy", o_t[:, 0:256], f_t[:, 0:256], reads=[f_b], writes=[o_b])
                        pt = tt - TS // 128
                        dma_out(st_nv[pt * 128:(pt + 1) * 128, t * 256:(t + 1) * 256], b["st"], f_t[:, 0:256], f_b, final=True, nonc=True)
                        dma_out(VNP[pt * 128:(pt + 1) * 128, t * 256:(t + 1) * 256], b["VNP"], o_t[:, 0:256], o_b, nonc=True)
                    else:
                        P.op("act", "copy", o_t[:, 0:256], ps[bk][:, 0:256], reads=[bps[bk]], writes=[o_b])
                        dma_out(VN[tt * 128:(tt + 1) * 128, t * 256:(t + 1) * 256], b["VN"], o_t[:, 0:256], o_b, nonc=True)
                        if tt >= 14:
                            for hr in range(2):
                                lrow = (tt - 14) * 2 + hr
                                dpos = 3 - lrow
                                dma_out(VNS[dpos * 64:(dpos + 1) * 64, t * 256:(t + 1) * 256], b["VNS"],
                                        o_t[hr * 64:(hr + 1) * 64, 0:256], o_b, nonc=True)
            P.dma("pool", "collective_compute", "AllGather", ALU.bypass, replica_groups=PAIRS,
                                                         ins=[KNS[:, :].opt()], outs=[KNSa[:, :].opt()],
                  reads=[b["KNS"]], writes=[b["KNSa"]], sembuf=b["KNSa"], inc=1)
            P.dma("pool", "collective_compute", "AllGather", ALU.bypass, replica_groups=PAIRS,
                                                         ins=[VNS[:, :].opt()], outs=[VNSa[:, :].opt()],
                  reads=[b["VNS"]], writes=[b["VNSa"]], sembuf=b["VNSa"], inc=1)
            flush_fin(); P.barrier()
            scopeA.close()
            with ExitStack() as s2:
                KT = s2.enter_context(nc.sbuf_tensor("naK", [64, 3328], BF16))
                b_KT = P.buf("naK")
                VA = [s2.enter_context(nc.sbuf_tensor(f"naV{i}", [128, 26, 128], BF16)) for i in range(2)]
                b_VA = [P.buf("naV0"), P.buf("naV1")]
                QT = s2.enter_context(nc.sbuf_tensor("naQ", [64, NT], BF16))
                b_QT = P.buf("naQ")
                MK = s2.enter_context(nc.sbuf_tensor("naM", [128, 18, 512], BF16))
                b_MK = P.buf("naM")
                P.dma("pool", "dma_start", out=MK[:], in_=masks.rearrange("t p q -> p t q"), writes=[b_MK], sembuf=b_MK)
                Tz32 = s2.enter_context(nc.sbuf_tensor("naTz32", [128, RR * 64], F32))
                b_Tz32 = P.buf("naTz32")
                Tzb = [s2.enter_context(nc.sbuf_tensor(f"naTzb{i}", [128, RR * 64], BF16)) for i in range(2)]
                b_Tzb = [P.buf("naTzb0"), P.buf("naTzb1")]
                P.op("pool", "memset", VA[0][:, :, 64:128], 1.0, writes=[b_VA[0]])
                P.op("pool", "memset", VA[1][:, :, 0:64], 1.0, writes=[b_VA[1]])
                for h in range(8):
                    m = h % 2
                    vc0 = 0 if m == 0 else 64
                    tzv = Tz32[:, :].rearrange("p (r q) -> p r q", q=64)
                    dma_nc(tzv[0:64, :, :], b_Tz32, bass.AP(rpbp.tensor, h * R2 * RW + RW + 16, [[1, 64], [RW, RR], [1, 64]]))
                    dma_nc(tzv[64:128, :, :], b_Tz32, bass.AP(rpbp.tensor, h * R2 * RW + 16, [[1, 64], [RW, RR], [1, 64]]), accum=True)
                    P.op("act", "copy", Tzb[m][:, :], Tz32[:, :], reads=[b_Tz32], writes=[b_Tzb[m]])
                    dma_in(KT[:, 0:2048], b_KT, KN[h, :, :], reads=[b["KN"]])
                    dma_in(KT[:, 2048:2304], b_KT, KNSa[h * 64:(h + 1) * 64, :], reads=[b["KNSa"]], accum=True)
                    dma_in(KT[:, 2304:2560], b_KT, KNSa[512 + h * 64:512 + (h + 1) * 64, :], reads=[b["KNSa"]], accum=True)
                    dma_in(KT[:, 2560:2816], b_KT, CKN[h, :, :], reads=[b_CKN], accum=True)
                    dma_in(KT[:, 2816:3328], b_KT, KNP[h, :, :], reads=[b["KNP"]], accum=True)
                    dma_in(QT[:, :], b_QT, QN[h, :, :], reads=[b["QN"]])
                    hs = slice(h * 64, (h + 1) * 64)
                    dma_nc(VA[m][:, 0:16, vc0:vc0 + 64], b_VA[m], VN[:, hs].rearrange("(c p) d -> p c d", p=128), reads=[b["VN"]], accum=True)
                    dma_nc(VA[m][:, 16:18, vc0:vc0 + 64], b_VA[m], VNSa[0:256, hs].rearrange("(c p) d -> p c d", p=128), reads=[b["VNSa"]], accum=True)
                    dma_nc(VA[m][:, 18:20, vc0:vc0 + 64], b_VA[m], VNSa[256:512, hs].rearrange("(c p) d -> p c d", p=128), reads=[b["VNSa"]], accum=True)
                    dma_nc(VA[m][:, 20:22, vc0:vc0 + 64], b_VA[m], c_nv[:, hs].rearrange("(c p) d -> p c d", p=128), q="pool", accum=True)
                    dma_nc(VA[m][:, 22:26, vc0:vc0 + 64], b_VA[m], VNP[:, hs].rearrange("(c p) d -> p c d", p=128), reads=[b["VNP"]], accum=True)

                    def ec(ti, m=m):
                        if ti < 6:
                            dr0 = 2 * ti + 7
                        elif ti < 14:
                            dr0 = 2 * (ti - 6) + 3
                        else:
                            dr0 = 2 * (6 + (ti - 14) % 2) + 3
                        r0_ = (RR - 1) - (dr0 + 4)
                        return (Tzb[m][:, r0_ * 64:(r0_ + 8) * 64], b_Tzb[m], MK[:, ti, :], b_MK)
                    kc = lambda c: (KT[:, c * 128:(c + 1) * 128], b_KT)
                    vc = lambda c: (VA[m][:, c, :], b_VA[m])
                    for j in range(4):
                        if j == 0:
                            own, es = list(range(0, 6)), list(range(0, 6))
                        elif j < 3:
                            own, es = list(range(4 * j - 2, 4 * j + 6)), list(range(6, 14))
                        else:
                            own, es = list(range(10, 20)), list(range(6, 12)) + [14, 15, 16, 17]
                        cks = own + [20, 21]
                        ecs = [ec(t_) for t_ in es] + [None, None]
                        attention(QT[:, j * 512:(j + 1) * 512], b_QT, 512, 64, [kc(c) for c in cks], [vc(c) for c in cks],
                                  0.125, m, hT[m * 64:(m + 1) * 64, 4 + h // 2, j * 512:(j + 1) * 512], [bh[4 + h // 2][j]], echunks=ecs)
                    for pb in range(2):
                        cks = [22 + 2 * pb, 23 + 2 * pb]
                        q0 = TS + pb * 256
                        attention(QT[:, q0:q0 + 256], b_QT, 256, 64, [kc(c) for c in cks], [vc(c) for c in cks],
                                  0.125, m, hT[m * 64:(m + 1) * 64, 4 + h // 2, q0:q0 + 256], [bh[4 + h // 2][4]])
            flush_fin(); P.barrier()
            with ExitStack() as s2:
                CA = s2.enter_context(nc.sbuf_tensor("mlC", [128, 2, 4864], BF16))
                b_CA = P.buf("mlC")
                KT = s2.enter_context(nc.sbuf_tensor("mlK", [96, 4864], BF16))
                b_KT = P.buf("mlK")
                VA = [s2.enter_context(nc.sbuf_tensor(f"mlV{i}", [128, 38, 128], BF16)) for i in range(2)]
                b_VA = [P.buf("mlV0"), P.buf("mlV1")]
                QT = s2.enter_context(nc.sbuf_tensor("mlQ", [96, NT], BF16))
                b_QT = P.buf("mlQ")
                wk_t = s2.enter_context(nc.sbuf_tensor("wukv_t", [128, 2, 1024], BF16))
                b_wk = P.buf("wukv")
                P.dma("pool", "dma_start", out=wk_t[:], in_=wukv.rearrange("(k p) n -> p k n", p=128), writes=[b_wk], sembuf=b_wk)
                P.op("pool", "memset", VA[0][:, :, 64:128], 1.0, writes=[b_VA[0]])
                P.op("pool", "memset", VA[1][:, :, 0:64], 1.0, writes=[b_VA[1]])
                for r in range(2):
                    dma_in(CA[:, :, r * TS:(r + 1) * TS], b_CA, KVCa[r * 288:r * 288 + 256, :].rearrange("(k p) n -> p k n", p=128),
                           reads=[b["KVCa"]], accum=(r > 0))
                dma_in(CA[:, :, 4096:4352], b_CA, CKM[0:256, :].rearrange("(k p) n -> p k n", p=128), reads=[b_CKM], accum=True)
                dma_in(CA[:, :, 4352:4864], b_CA, CKP[0:256, :].rearrange("(k p) n -> p k n", p=128), reads=[b["CKP"]], accum=True)
                for h in range(8):
                    m = h % 2
                    vc0 = 0 if m == 0 else 64
                    dma_in(KT[64:96, 0:2048], b_KT, KVCa[256:288, :], reads=[b["KVCa"]])
                    dma_in(KT[64:96, 2048:4096], b_KT, KVCa[544:576, :], reads=[b["KVCa"]], accum=True)
                    dma_in(KT[64:96, 4096:4352], b_KT, CKM[256:288, :], reads=[b_CKM], accum=True)
                    dma_in(KT[64:96, 4352:4864], b_KT, CKP[256:288, :], reads=[b["CKP"]], accum=True)
                    dma_in(QT[:, :], b_QT, QM[h, :, :], reads=[b["QM"]])
                    for cb in range(10):
                        c0 = cb * 512
                        n = min(512, 4864 - c0)
                        bk = nbank()
                        mm_acc(bk, 0, 64, n, [(wk_t[:, k, h * 64:(h + 1) * 64], CA[:, k, c0:c0 + n]) for k in range(2)], [b_wk, b_CA])
                        P.op("act", "copy", KT[0:64, c0:c0 + n], ps[bk][0:64, 0:n], reads=[bps[bk]], accum=[b_KT])
                    for g8 in range(0, 38, 8):
                        ng = min(8, 38 - g8)
                        bk = nbank()
                        for ci in range(ng):
                            ck = g8 + ci
                            for k in range(2):
                                P.op("pe", "matmul",
                                    ps[bk][:, ci * 64:(ci + 1) * 64], CA[:, k, ck * 128:(ck + 1) * 128],
                                    wk_t[:, k, 512 + h * 64:512 + (h + 1) * 64], start=(k == 0), stop=(k == 1),
                                    reads=[b_wk, b_CA], writes=[bps[bk]], inc=(ci == ng - 1 and k == 1))
                        P.op("dve", "tensor_copy",
                            VA[m][:, g8:g8 + ng, vc0:vc0 + 64], ps[bk][:, 0:ng * 64].rearrange("p (c d) -> p c d", d=64),
                            reads=[bps[bk]], accum=[b_VA[m]])
                    kc = lambda c: (KT[:, c * 128:(c + 1) * 128], b_KT)
                    vc = lambda c: (VA[m][:, c, :], b_VA[m])
                    sc = 96.0 ** -0.5
                    for j in range(4):
                        cks = list(range(34))
                        attention(QT[:, j * 512:(j + 1) * 512], b_QT, 512, 96, [kc(c) for c in cks], [vc(c) for c in cks],
                                  sc, m, hT[m * 64:(m + 1) * 64, h // 2, j * 512:(j + 1) * 512], [bh[h // 2][j]])
                    for pb in range(2):
                        cks = [34 + 2 * pb, 35 + 2 * pb]
                        q0 = TS + pb * 256
                        attention(QT[:, q0:q0 + 256], b_QT, 256, 96, [kc(c) for c in cks], [vc(c) for c in cks],
                                  sc, m, hT[m * 64:(m + 1) * 64, h // 2, q0:q0 + 256], [bh[h // 2][4]])
                    flush_fin()
            flush_fin(); P.barrier()
            dbg_dump("AT0", hT[:], (128, 8, NT), [bh[c][t] for c in range(8) for t in range(NB)])
            open_pools(stack)
            out_proj(woa, 2)
            norm_mod(3, 4)
            ffn(0)

        def layer1():
            compute_mod(1)
            norm_mod(0, 1)
            b = {n: P.buf(n) for n in ("QG", "KG", "KGa", "KGP", "VG", "VGa", "VGP", "st")}
            with ExitStack() as s2:
                rg = s2.enter_context(nc.sbuf_tensor("rg", [128, 2, 512], F32))
                b_rg = P.buf("rg")
                tq = s2.enter_context(nc.sbuf_tensor("tq", [128, 4, 512], F32))
                b_tq = P.buf("tq")
                for isk in range(2):
                    nh = 2 if isk else 8
                    for hh in range(0, nh, 2):
                        c_main = (WC_K if isk else WC_Q) + hh * 128
                        c_sw = (WC_KS if isk else WC_QS) + hh * 128
                        wm, wmb = load_w(wview(wc, c_main, 256), 8, 256)
                        wsw, wswb = load_w(wview(wc, c_sw, 256), 8, 256)
                        for tb in range(NB):
                            cols = slice(tb * 512, (tb + 1) * 512)
                            if tb < 4:
                                dma_in(rg[:], b_rg, ropeg[:, :, cols])
                                g_, gs_ = ("ggk", "ggks") if isk else ("ggq", "ggqs")
                                P.op("dve", "tensor_scalar", tq[:, 2 * isk, :], rg[:, 0, :], V(g_, 0, 1), None, ALU.mult,
                                     reads=[b_rg, b_vec], writes=[b_tq])
                                P.op("dve", "tensor_scalar", tq[:, 2 * isk + 1, :], rg[:, 1, :], V(gs_, 0, 1), None, ALU.mult,
                                     reads=[b_rg, b_vec], accum=[b_tq])
                            for h2 in range(2):
                                h = hh + h2
                                b1 = nbank()
                                proj_h(wm, wmb, h2 * 128, 128, tb, b1)
                                r_t, r_b = rstd_block(lambda c, b1=b1: ps[b1][:, 0:512], 1, tb, 512, 1.0 / 128, lambda c, b1=b1: [bps[b1]])
                                o_t, o_b = bft.get()
                                if tb < 4:
                                    b2 = nbank()
                                    proj_h(wsw, wswb, h2 * 128, 128, tb, b2)
                                    u_t, u_b = f32t.get()
                                    P.op("dve", "tensor_tensor", u_t[:], ps[b1][:, 0:512], tq[:, 2 * isk, :], ALU.mult,
                                         reads=[bps[b1], b_tq], writes=[u_b])
                                    v_t, v_b = f32t.get()
                                    P.op("dve", "tensor_tensor", v_t[:], ps[b2][:, 0:512], tq[:, 2 * isk + 1, :], ALU.mult,
                                         reads=[bps[b2], b_tq], writes=[v_b])
                                    P.op("pool", "tensor_tensor", u_t[:], u_t[:], v_t[:], ALU.add,
                                         reads=[v_b], writes=[u_b])
                                    P.op("dve", "tensor_tensor", o_t[:], u_t[:], r_t[:], ALU.mult,
                                         reads=[u_b, r_b], writes=[o_b])
                                    if isk:
                                        dma_out(KG[h * 128:(h + 1) * 128, cols], b["KG"], o_t[:], o_b)
                                    else:
                                        dma_out(QG[h, :, cols], b["QG"], o_t[:], o_b)
                                else:
                                    g_ = "ggk" if isk else "ggq"
                                    u_t, u_b = f32t.get()
                                    P.op("dve", "scalar_tensor_tensor",
                                        u_t[:], ps[b1][:, 0:512], V(g_, 0, 1), r_t[:], ALU.mult, ALU.mult, reads=[bps[b1], r_b, b_vec], writes=[u_b])
                                    P.op("act", "copy", o_t[:], u_t[:], reads=[u_b], writes=[o_b])
                                    if isk:
                                        dma_out(KGP[h, :, :], b["KGP"], o_t[:], o_b)
                                        store_tok_major(u_t, u_b, 128, 4, lambda t, h=h: st_gk[t * 128:(t + 1) * 128, h * 128:(h + 1) * 128], b["st"])
                                    else:
                                        dma_out(QG[h, :, cols], b["QG"], o_t[:], o_b)
                wt, wb_ = load_w(wview(wc, WC_V, 256), 8, 256)
                for tt in range(NT // 128):
                    bk = nbank()
                    tb = tt // 4
                    mm_acc(bk, 0, 128, 256, [(hT[:, k, tt * 128:(tt + 1) * 128], wt[:, k, :]) for k in range(8)],
                           [wb_] + [bh[k][tb] for k in range(8)])
                    o_t, o_b = bft.get()
                    if tb == 4:
                        f_t, f_b = f32t.get()
                        P.op("act", "copy", f_t[:, 0:256], ps[bk][:, 0:256], reads=[bps[bk]], writes=[f_b])
                        P.op("dve", "tensor_copy", o_t[:, 0:256], f_t[:, 0:256], reads=[f_b], writes=[o_b])
                        pt = tt - TS // 128
                        dma_out(st_gv[pt * 128:(pt + 1) * 128, :], b["st"], f_t[:, 0:256], f_b, final=True)
                        dma_out(VGP[pt * 128:(pt + 1) * 128, :], b["VGP"], o_t[:, 0:256], o_b)
                    else:
                        P.op("act", "copy", o_t[:, 0:256], ps[bk][:, 0:256], reads=[bps[bk]], writes=[o_b])
                        dma_out(VG[tt * 128:(tt + 1) * 128, :], b["VG"], o_t[:, 0:256], o_b)
                flush_fin(); P.barrier()
            P.dma("pool", "collective_compute", "AllGather", ALU.bypass, replica_groups=PAIRS,
                                                         ins=[KG[:, :].opt()], outs=[KGa[:, :].opt()],
                  reads=[b["KG"]], writes=[b["KGa"]], sembuf=b["KGa"], inc=1)
            P.dma("pool", "collective_compute", "AllGather", ALU.bypass, replica_groups=PAIRS,
                                                         ins=[VG[:, :].opt()], outs=[VGa[:, :].opt()],
                  reads=[b["VG"]], writes=[b["VGa"]], sembuf=b["VGa"], inc=1)
            with ExitStack() as s2:
                KT = s2.enter_context(nc.sbuf_tensor("gqK", [128, 4864], BF16))
                b_KT = P.buf("gqK")
                VT = s2.enter_context(nc.sbuf_tensor("gqV", [128, 38, 128], BF16))
                b_VT = P.buf("gqV")
                QTs = [s2.enter_context(nc.sbuf_tensor(f"gqQ{i}", [128, NT], BF16)) for i in range(2)]
                b_QTs = [P.buf("gqQ0"), P.buf("gqQ1")]
                for h in range(8):
                    g = h // 4
                    if h % 4 == 0:
                        for r in range(2):
                            dma_in(KT[:, r * TS:(r + 1) * TS], b_KT, KGa[r * 256 + g * 128:r * 256 + (g + 1) * 128, :], reads=[b["KGa"]], accum=(r > 0))
                        dma_in(KT[:, 4096:4352], b_KT, CKG[g, :, :], reads=[b_CKG], accum=True)
                        dma_in(KT[:, 4352:4864], b_KT, KGP[g, :, :], reads=[b["KGP"]], accum=True)
                        gs = slice(g * 128, (g + 1) * 128)
                        dma_nc(VT[:, 0:32, :], b_VT, VGa[:, gs].rearrange("(c p) d -> p c d", p=128), reads=[b["VGa"]])
                        dma_nc(VT[:, 32:34, :], b_VT, c_gv[:, gs].rearrange("(c p) d -> p c d", p=128), q="pool", accum=True)
                        dma_nc(VT[:, 34:38, :], b_VT, VGP[:, gs].rearrange("(c p) d -> p c d", p=128), reads=[b["VGP"]], accum=True)
                    QT, b_QT = QTs[h % 2], b_QTs[h % 2]
                    dma_in(QT[:, :], b_QT, QG[h, :, :], reads=[b["QG"]])
                    kc = lambda c: (KT[:, c * 128:(c + 1) * 128], b_KT)
                    vc = lambda c: (VT[:, c, :], b_VT)
                    sc = 128.0 ** -0.5
                    for j in range(4):
                        cks = list(range(34))
                        attention(QT[:, j * 512:(j + 1) * 512], b_QT, 512, 128, [kc(c) for c in cks], [vc(c) for c in cks],
                                  sc, 2, hT[:, h, j * 512:(j + 1) * 512], [bh[h][j]])
                    for pb in range(2):
                        cks = [34 + 2 * pb, 35 + 2 * pb]
                        q0 = TS + pb * 256
                        attention(QT[:, q0:q0 + 256], b_QT, 256, 128, [kc(c) for c in cks], [vc(c) for c in cks],
                                  sc, 2, hT[:, h, q0:q0 + 256], [bh[h][4]])
            flush_fin(); P.barrier()
            dbg_dump("AT1", hT[:], (128, 8, NT), [bh[c][t] for c in range(8) for t in range(NB)])
            out_proj(woc, 2)
            norm_mod(3, 4)
            ffn(1)

        layer0()
        layer1()

        b_y = P.buf("y")
        go, _ = VMAP["gfin"]
        for tb in range(NB):
            cols = slice(tb * 512, (tb + 1) * 512)
            r_t, r_b = rstd_block(lambda c: xT[:, c, cols], 8, tb, 512, 1.0 / D, lambda c: [bx[c][tb]])
            for c in range(8):
                P.op("dve", "scalar_tensor_tensor", xT[:, c, cols], xT[:, c, cols], vec[:, go + c:go + c + 1], r_t[:], ALU.mult, ALU.mult,
                     reads=[r_b, b_vec], writes=[bx[c][tb]])
            for t4 in range(4):
                tok = tb * 512 + t4 * 128
                s_t, s_b = stg.get()
                for half in range(2):
                    bk = nbank()
                    for c4 in range(4):
                        c = half * 4 + c4
                        P.op("pe", "transpose", ps[bk][:, c4 * 128:(c4 + 1) * 128], xT[:, c, tok:tok + 128], ident,
                             reads=[bx[c][tb], b_cst], writes=[bps[bk]], inc=(c4 == 3))
                    if half == 0:
                        P.op("dve", "tensor_copy", s_t[:, 0:512], ps[bk][:, 0:512], reads=[bps[bk]], writes=[s_b])
                    else:
                        P.op("act", "copy", s_t[:, 512:1024], ps[bk][:, 0:512], reads=[bps[bk]], accum=[s_b])
                if tok < TS:
                    dma_out(y_s[tok:tok + 128, :], b_y, s_t[:], s_b, final=True)
                else:
                    dma_out(y_p[tok - TS:tok - TS + 128, :], b_y, s_t[:], s_b, final=True)

        P.emit(nc, stack)
    return nc


def _rope_tables(tok_rows, tok_cols, rot_dim):
    axis_dim = rot_dim // 2
    inv = (10000.0 ** (-np.arange(0, axis_dim, 2, dtype=np.float32) / axis_dim)).astype(np.float32)
    ang = np.concatenate([tok_rows[:, None].astype(np.float32) * inv, tok_cols[:, None].astype(np.float32) * inv], axis=-1)
    c, s = np.cos(ang).astype(np.float32), np.sin(ang).astype(np.float32)
    cos2 = np.concatenate([c, c], axis=1).T
    sins = np.concatenate([-s, s], axis=1).T
    return np.ascontiguousarray(np.stack([cos2, sins], axis=1)).astype(np.float32)


def _na_masks(half):
    m = np.full((18, 128, 512), 8.0 * MASKNEG, np.float32)
    qc = np.arange(64)
    cs = np.clip(qc - 8, 0, 48)
    kcol = np.arange(64)
    colok = (kcol[:, None] >= cs[None, :]) & (kcol[:, None] < cs[None, :] + 16)

    def glob(l):
        return l if half == 0 else 63 - l

    def tile(ti, krows, qrows, partner_rank=None):
        for a, kl in enumerate(krows):
            for i, ql in enumerate(qrows):
                if kl is None or kl < 0 or kl > 35:
                    continue
                if kl >= 32:
                    pl = 31 - (kl - 32)
                    kg = (63 - pl) if half == 0 else pl
                    if partner_rank is not None and partner_rank != (1 - half):
                        continue
                else:
                    kg = glob(kl)
                    if partner_rank is not None:
                        continue
                qg = glob(ql)
                rs = min(max(qg - 4, 0), 56)
                if rs <= kg < rs + 8:
                    blk = m[ti, a * 64:(a + 1) * 64, i * 64:(i + 1) * 64]
                    blk[colok] = 0.0
    for c in range(6):
        tile(c, [2 * c, 2 * c + 1], list(range(8)))
    for c in range(8):
        j = 1
        tile(6 + c, [8 * j - 4 + 2 * c, 8 * j - 4 + 2 * c + 1], [8 * j + i for i in range(8)])
    for e in range(2):
        tile(14 + e, [32 + 2 * e, 33 + 2 * e], [24 + i for i in range(8)], partner_rank=0)
        tile(16 + e, [32 + 2 * e, 33 + 2 * e], [24 + i for i in range(8)], partner_rank=1)
    return m


def _fm(v, n):
    return np.ascontiguousarray(np.asarray(v, np.float32).reshape(n, 128).T)


def _prep(inp):
    f = lambda k: np.asarray(inp[k], np.float32)
    w_in_a = f("w_in_a")[0]
    sw32 = np.r_[16:32, 0:16]
    wa = np.concatenate([w_in_a[:, 0:512], w_in_a[:, 512:544], w_in_a[:, 512:544][:, sw32], w_in_a[:, 544:2080]], axis=1)
    uq = f("mla_w_uq")[0].reshape(256, 8, 96)
    uq_sw = uq.copy()
    uq_sw[:, :, 64:96] = uq[:, :, 64:96][:, :, sw32]
    wuq = np.concatenate([uq.reshape(256, 768), uq_sw.reshape(256, 768)], axis=1)
    ukv = f("mla_w_ukv")[0].reshape(256, 8, 128)
    wukv = np.concatenate([ukv[:, :, 0:64].reshape(256, 512), ukv[:, :, 64:128].reshape(256, 512)], axis=1)
    w_in_c = f("w_in_c")[0]
    sw128 = np.r_[64:128, 0:64]
    q = w_in_c[:, 0:1024].reshape(D, 8, 128)
    k = w_in_c[:, 1024:1280].reshape(D, 2, 128)
    wc = np.concatenate([w_in_c, q[:, :, sw128].reshape(D, 1024), k[:, :, sw128].reshape(D, 256)], axis=1)
    shared = dict(
        wmod=f("w_mod"), wa=np.ascontiguousarray(wa), wuq=np.ascontiguousarray(wuq), wukv=np.ascontiguousarray(wukv),
        woa=f("w_out_a")[0], wc=np.ascontiguousarray(wc), woc=f("w_out_c")[0], wfi=f("w_ffn_in"), wfo=f("w_ffn_out"),
    )
    cstm = np.zeros((128, 640), np.float32)
    for m_ in range(128):
        cstm[64 * (m_ // 64) + 63 - (m_ % 64), 512 + m_] = 1.0
    cstm[:, 0:128] = np.eye(128, dtype=np.float32)
    cstm[64, 128:256] = 1.0
    cstm[0, 256:384] = 1.0
    cstm[:, 384:512] = 1.0
    shared["cst"] = cstm
    vecs = np.zeros((128, NVEC), np.float32)

    def put(nm, arr):
        o, n = VMAP[nm]
        vecs[:, o:o + n] = arr
    bm = f("b_mod")
    for l in range(2):
        put(f"bmod{l}", _fm(bm[l], 48))
        put(f"gmix{l}", _fm(f("norm_mix")[l], 8))
        put(f"gffn{l}", _fm(f("norm_ffn")[l], 8))
    put("gfin", _fm(f("norm_final"), 8))
    put("gq", _fm(f("mla_q_norm")[0], 2))
    put("gkv", _fm(f("mla_kv_norm")[0], 2))
    gq, gk = f("gqa_q_norm")[0], f("gqa_k_norm")[0]
    put("eps", np.full((128, 1), EPS, np.float32))
    put("ggq", gq[:, None]); put("ggqs", gq[sw128][:, None]); put("ggk", gk[:, None]); put("ggks", gk[sw128][:, None])
    rpb = f("na_rpb")[0]
    xs_all, xp_all = f("x_sample"), f("x_prompt")
    c_all, c_ctx = f("c"), f("c_ctx")
    in_maps, perms = [], []
    for core in range(8):
        bsz, half = core // 2, core % 2
        lrows = np.arange(32)
        grows = lrows if half == 0 else 63 - lrows
        perm = (grows[:, None] * 64 + np.arange(64)[None, :]).reshape(-1)
        perms.append(perm)
        d = dict(shared)
        d["xs"] = np.ascontiguousarray(xs_all[bsz][perm])
        d["xp"] = np.ascontiguousarray(xp_all[2 * core:2 * core + 2].reshape(TP, D))
        v = vecs.copy()
        o, n = VMAP["cond"]
        cd = np.stack([_fm(c_all[bsz], 8), _fm(c_ctx, 8)], axis=2).reshape(128, 16)
        v[:, o:o + n] = cd
        d["vec"] = v
        d["c_ckv"] = f("cache_mla_ckv")[bsz, 0]
        d["c_kr"] = f("cache_mla_krope")[bsz, 0]
        d["c_nk"] = np.ascontiguousarray(f("cache_na_k")[bsz, 0].reshape(256, 512))
        d["c_nv"] = np.ascontiguousarray(f("cache_na_v")[bsz, 0].reshape(256, 512))
        d["c_gk"] = np.ascontiguousarray(f("cache_gqa_k")[bsz, 0].reshape(256, 256))
        d["c_gv"] = np.ascontiguousarray(f("cache_gqa_v")[bsz, 0].reshape(256, 256))
        trow = np.repeat(grows, 64)
        tcol = np.tile(np.arange(64), 32)
        d["ropem"] = _rope_tables(trow, tcol, 32)
        d["ropeg"] = _rope_tables(trow, tcol, 128)
        rp = np.zeros((8, R2, RW), np.float32)
        r_use = rpb if half == 0 else rpb[:, ::-1, :]
        for dr in range(15):
            rp[:, 1 + (RR - 1) - (dr + 4), 64:95] = r_use[:, dr, ::-1]
        d["rpbp"] = rp
        d["masks"] = _na_masks(half)
        in_maps.append(d)
    return in_maps, perms


_NC_CACHE = {}


def kernel(**inputs):
    in_maps, perms = _prep(inputs)
    if "nc" not in _NC_CACHE:
        _NC_CACHE["nc"] = build()
    nc = _NC_CACHE["nc"]
    res = run_bass_kernel_spmd(nc, in_maps, core_ids=list(range(8)))
    R = res.results
    y_prompt = np.zeros((16, 256, D), np.float32)
    y_sample = np.zeros((4, 4096, D), np.float32)
    s_ckv = np.zeros((16, 1, 256, 256), np.float32)
    s_kr = np.zeros((16, 1, 256, 32), np.float32)
    s_nk = np.zeros((16, 1, 256, 8, 64), np.float32)
    s_nv = np.zeros((16, 1, 256, 8, 64), np.float32)
    s_gk = np.zeros((16, 1, 256, 2, 128), np.float32)
    s_gv = np.zeros((16, 1, 256, 2, 128), np.float32)
    for core in range(8):
        r = R[core]
        y_sample[core // 2][perms[core]] = r["y_s"]
        sl = slice(2 * core, 2 * core + 2)
        y_prompt[sl] = r["y_p"].reshape(2, 256, D)
        s_ckv[sl, 0] = r["st_ckv"].reshape(2, 256, 256)
        s_kr[sl, 0] = r["st_kr"].reshape(2, 256, 32)
        s_nk[sl, 0] = r["st_nk"].reshape(2, 256, 8, 64)
        s_nv[sl, 0] = r["st_nv"].reshape(2, 256, 8, 64)
        s_gk[sl, 0] = r["st_gk"].reshape(2, 256, 2, 128)
        s_gv[sl, 0] = r["st_gv"].reshape(2, 256, 2, 128)
    return (y_prompt, y_sample, s_ckv, s_kr, s_nk, s_nv, s_gk, s_gv)
```

```python
from contextlib import ExitStack
import numpy as np
import concourse.bass as bass
import concourse.mybir as mybir
from concourse.bass_utils import run_bass_kernel_spmd

F32 = mybir.dt.float32
BF16 = mybir.dt.bfloat16
AF = mybir.ActivationFunctionType
ALU = mybir.AluOpType

D = 1024
TS = 2048
TP = 512
NT = TS + TP
NB = NT // 512
DFF = 2816
NFF = DFF // 128
EPS = 1e-6
MASKNEG = -200.0
RW = 160
RR = 23
R2 = RR + 1


class Buf:
    __slots__ = ("name", "w", "r", "pr", "semkey", "semcnt", "semkey2", "semcnt2", "excl")

    def __init__(self, name, excl=False):
        self.name = name
        self.excl = excl
        self.w = []
        self.r = []
        self.pr = []
        self.semkey = None
        self.semcnt = 0
        self.semkey2 = None
        self.semcnt2 = 0


class Stream:
    def __init__(self, name):
        self.name = name
        self.ops = []
        self.cnt = 0
        self.semkey = "eng_" + name
        self.seen = {}


class Prog:
    def __init__(self):
        self.streams = {n: Stream(n) for n in ("pe", "act", "dve", "pool", "sp")}
        self.semkeys = ["eng_pe", "eng_act", "eng_dve", "eng_pool"]
        self.nbuf = 0
        self.outs = []
        self.semcur = {}

    def buf(self, name=None, excl=False):
        self.nbuf += 1
        return Buf(f"{name or 'b'}{self.nbuf}", excl)

    def _need(self, st, ev):
        k, v = ev
        if st.name == "pe" and k == "eng_pe":
            return
        if k in self.semcur:
            v = max(v, self.semcur[k])
        if st.seen.get(k, 0) >= v:
            return
        st.seen[k] = v
        st.ops.append(("wait", k, v))

    def _sync(self, st, reads, writes, accum):
        reads = list(reads)
        writes = list(writes)
        accum = list(accum)
        r2 = [b for b in reads if not b.excl]
        w2 = writes + [b for b in reads if b.excl]
        for b in r2:
            for ev in b.w:
                self._need(st, ev)
        for b in w2:
            for ev in b.w:
                self._need(st, ev)
            for ev in b.r:
                self._need(st, ev)
        for b in accum:
            for ev in b.r:
                self._need(st, ev)
            if b.r:
                for ev in b.w:
                    self._need(st, ev)
            for ev in b.pr:
                self._need(st, ev)
        return r2, w2, accum

    def _record(self, ev, r2, w2, accum):
        for b in r2:
            b.r.append(ev)
        for b in w2:
            b.pr = b.w + b.r
            b.w = [ev]
            b.r = []
        for b in accum:
            if b.r:
                b.pr = b.w + b.r
                b.w = []
                b.r = []
            b.w.append(ev)

    def op(self, stname, meth, *args, reads=(), writes=(), accum=(), inc=True, **kw):
        st = self.streams[stname]
        inc = True
        r2, w2, ac = self._sync(st, reads, writes, accum)
        ev = (st.semkey, st.cnt + 1)
        fn = (meth, args, kw, False)
        if inc:
            st.cnt += 1
            st.ops.append(("op", fn, (st.semkey, 1)))
        else:
            st.ops.append(("op", fn, None))
        self._record(ev, r2, w2, ac)
        return ev

    def dma(self, qname, meth, *args, reads=(), writes=(), accum=(), sembuf=None, inc=16, nonc=False, **kw):
        st = self.streams[qname]
        r2, w2, ac = self._sync(st, reads, writes, accum)
        if qname == "pool":
            if sembuf.semkey2 is None:
                sembuf.semkey2 = f"s{len(self.semkeys)}_{sembuf.name}"
                self.semkeys.append(sembuf.semkey2)
            sembuf.semcnt2 += inc
            key, cnt = sembuf.semkey2, sembuf.semcnt2
        else:
            if sembuf.semkey is None:
                sembuf.semkey = f"d{len(self.semkeys)}_{sembuf.name}"
                self.semkeys.append(sembuf.semkey)
            sembuf.semcnt += inc
            key, cnt = sembuf.semkey, sembuf.semcnt
        self.semcur[key] = cnt
        ev = (key, cnt)
        st.ops.append(("op", (meth, args, kw, nonc), (key, inc)))
        self._record(ev, r2, w2, ac)
        return ev

    def barrier(self):
        for st in self.streams.values():
            for o in self.streams.values():
                if o.name != "sp" and o.cnt:
                    self._need(st, (o.semkey, o.cnt))
            for k, v in self.semcur.items():
                self._need(st, (k, v))

    def emit(self, nc, stack):
        st = self.streams["sp"]
        for ev in self.outs:
            self._need(st, ev)
        sems = {k: stack.enter_context(nc.semaphore(k)) for k in self.semkeys}
        block = stack.enter_context(nc.Block())

        def replay(s):
            def run(eng):
                for o in s.ops:
                    if o[0] == "wait":
                        eng.wait_ge(sems[o[1]], o[2])
                    else:
                        meth, args, kw, nonc = o[1]
                        if nonc:
                            with nc.allow_non_contiguous_dma(reason="strided"):
                                ins = getattr(eng, meth)(*args, **kw)
                        else:
                            ins = getattr(eng, meth)(*args, **kw)
                        if o[2] is not None:
                            ins.then_inc(sems[o[2][0]], o[2][1])
            return run

        for name, deco in (("pe", block.tensor), ("act", block.scalar), ("dve", block.vector),
                           ("pool", block.gpsimd), ("sp", block.sync)):
            if self.streams[name].ops:
                deco(replay(self.streams[name]))


def _vecmap():
    m = {}
    o = 0
    for nm, n in (("bmod0", 48), ("bmod1", 48), ("gmix0", 8), ("gmix1", 8), ("gffn0", 8), ("gffn1", 8),
                  ("gfin", 8), ("gq", 2), ("gkv", 2), ("ggq", 1), ("ggqs", 1), ("ggk", 1), ("ggks", 1),
                  ("cond", 16), ("eps", 1)):
        m[nm] = (o, n)
        o += n
    return m, o


VMAP, NVEC = _vecmap()

WA_CQ, WA_CKV, WA_KR, WA_KRS, WA_NQ, WA_NK, WA_NV, WA_N = 0, 256, 512, 544, 576, 1088, 1600, 2112
WC_Q, WC_K, WC_V, WC_QS, WC_KS, WC_N = 0, 1024, 1280, 1536, 2560, 2816


def build(debug_outs=()):
    nc = bass.Bass("TRN2", target_bir_lowering=False)
    P = Prog()

    def din(name, shape):
        return nc.dram_tensor(name, list(shape), F32, kind="ExternalInput").ap()

    def dout(name, shape):
        return nc.dram_tensor(name, list(shape), F32, kind="ExternalOutput").ap()

    def dscr(name, shape, dt=BF16):
        if name in debug_outs:
            return nc.dram_tensor(name, list(shape), dt, kind="ExternalOutput").ap()
        return nc.dram_tensor(name, list(shape), dt).ap()

    xs = din("xs", (TS, D))
    xp = din("xp", (TP, D))
    vec_d = din("vec", (128, NVEC))
    cst_d = din("cst", (128, 640))
    c_ckv = din("c_ckv", (256, 256))
    c_kr = din("c_kr", (256, 32))
    c_nk = din("c_nk", (256, 512))
    c_nv = din("c_nv", (256, 512))
    c_gk = din("c_gk", (256, 256))
    c_gv = din("c_gv", (256, 256))
    wmod = din("wmod", (2, D, 6 * D))
    wa = din("wa", (D, WA_N))
    wuq = din("wuq", (256, 1536))
    wukv = din("wukv", (256, 1024))
    woa = din("woa", (D, D))
    wc = din("wc", (D, WC_N))
    woc = din("woc", (D, D))
    wfi = din("wfi", (2, D, 2 * DFF))
    wfo = din("wfo", (2, DFF, D))
    ropem = din("ropem", (32, 2, TS))
    ropeg = din("ropeg", (128, 2, TS))
    rpbp = din("rpbp", (8, R2, RW))
    masks = din("masks", (18, 128, 512))
    y_s = dout("y_s", (TS, D))
    y_p = dout("y_p", (TP, D))
    st_ckv = dout("st_ckv", (TP, 256))
    st_kr = dout("st_kr", (TP, 32))
    st_nk = dout("st_nk", (TP, 512))
    st_nv = dout("st_nv", (TP, 512))
    st_gk = dout("st_gk", (TP, 256))
    st_gv = dout("st_gv", (TP, 256))
    QM = dscr("QM", (8, 96, NT))
    KVC = dscr("KVC", (288, TS))
    KVCa = dscr("KVCa", (576, TS))
    CKM = dscr("CKM", (288, 256))
    CKP = dscr("CKP", (288, TP))
    QN = dscr("QN", (8, 64, NT))
    KN = dscr("KN", (8, 64, TS))
    KNP = dscr("KNP", (8, 64, TP))
    KNS = dscr("KNS", (512, 256))
    KNSa = dscr("KNSa", (1024, 256))
    CKN = dscr("CKN", (8, 64, 256))
    VN = dscr("VN", (TS, 512))
    VNP = dscr("VNP", (TP, 512))
    VNS = dscr("VNS", (256, 512))
    VNSa = dscr("VNSa", (512, 512))
    QG = dscr("QG", (8, 128, NT))
    KG = dscr("KG", (256, TS))
    KGa = dscr("KGa", (512, TS))
    KGP = dscr("KGP", (2, 128, TP))
    CKG = dscr("CKG", (2, 128, 256))
    VG = dscr("VG", (TS, 256))
    VGa = dscr("VGa", (2 * TS, 256))
    VGP = dscr("VGP", (TP, 256))
    PAIRS = [[0, 1], [2, 3], [4, 5], [6, 7]]

    with ExitStack() as stack:
        def sb(name, shape, dt):
            return stack.enter_context(nc.sbuf_tensor(name, list(shape), dt))

        xT = sb("xT", (128, 8, NT), F32)
        hT = sb("hT", (128, 8, NT), BF16)
        bx = [[P.buf("x") for _ in range(NB)] for _ in range(8)]
        bh = [[P.buf("h") for _ in range(NB)] for _ in range(8)]
        cst = sb("cst_sb", (128, 640), F32)
        b_cst = P.buf("cst")
        ident = cst[:, 0:128]
        sel = [cst[:, 128:256], cst[:, 256:384]]
        onesb = sb("onesb", (128, 128), BF16)
        b_onesb = P.buf("onesb")
        identb = sb("identb", (128, 128), BF16)
        b_identb = P.buf("identb")
        J8b = sb("J8b", (128, 128), BF16)
        b_J8b = P.buf("J8b")
        vec = sb("vec_sb", (128, NVEC), F32)
        b_vec = P.buf("vec")
        silc = sb("silc", (128, 16), F32)
        b_silc = P.buf("silc")
        modv = sb("modv", (128, 2, 48), F32)
        b_modv = P.buf("modv")
        dm = sb("dm", (128, 2, 6, 8), F32)
        b_dm = P.buf("dm")
        rec = [sb("rec0", (128, 512), F32), sb("rec1", (128, 512), F32)]
        b_rec = [P.buf("rec0"), P.buf("rec1")]
        ps = [stack.enter_context(nc.psum_tensor(f"ps{i}", [128, 512], F32)) for i in range(8)]
        bps = [P.buf(f"ps{i}", excl=True) for i in range(8)]
        rr = {"ps": 0}

        def nbank():
            i = rr["ps"]
            rr["ps"] = (i + 1) % 8
            return i

        def V(nm, i=0, n=1):
            o, _ = VMAP[nm]
            return vec[:, o + i:o + i + n]

        uniq = {"n": 0}

        class Pool_:
            def __init__(self, name, shape, dt, n, stk=None):
                stk = stack if stk is None else stk
                uniq["n"] += 1
                self.t = [stk.enter_context(nc.sbuf_tensor(f"{name}{i}_{uniq['n']}", list(shape), dt)) for i in range(n)]
                self.b = [P.buf(f"{name}{i}") for i in range(n)]
                self.i = 0

            def get(self):
                i = self.i
                self.i = (i + 1) % len(self.t)
                return self.t[i], self.b[i]

        f32t = Pool_("f32t", (128, 512), F32, 5)
        rsd = Pool_("rsd", (128, 512), F32, 2)
        bft = Pool_("bft", (128, 512), BF16, 4)
        pl = {}

        def open_pools(stk):
            pl["w32"] = Pool_("w32", (128, 2048), F32, 2, stk)
            pl["wbf"] = Pool_("wbf", (128, 2048), BF16, 2, stk)
            pl["stg"] = Pool_("stg", (128, 1024), F32, 2, stk)

        class _PL:
            def __init__(self, k):
                self.k = k

            def get(self):
                return pl[self.k].get()

        w32, wbf, stg = _PL("w32"), _PL("wbf"), _PL("stg")
        scopeA = ExitStack()
        open_pools(scopeA)

        def dma_in(dst_ap, dst_b, src_ap, q="sp", reads=(), accum=False):
            if accum:
                return P.dma(q, "dma_start", out=dst_ap, in_=src_ap, reads=reads, accum=[dst_b], sembuf=dst_b)
            return P.dma(q, "dma_start", out=dst_ap, in_=src_ap, reads=reads, writes=[dst_b], sembuf=dst_b)

        def dma_nc(dst_ap, dst_b, src_ap, q="sp", reads=(), accum=False):
            if accum:
                return P.dma(q, "dma_start", out=dst_ap, in_=src_ap, reads=reads, accum=[dst_b], sembuf=dst_b, nonc=True)
            return P.dma(q, "dma_start", out=dst_ap, in_=src_ap, reads=reads, writes=[dst_b], sembuf=dst_b, nonc=True)

        def dma_out(dst_ap, dst_b, src_ap, src_b, q="sp", final=False, nonc=False):
            ev = P.dma(q, "dma_start", out=dst_ap, in_=src_ap, reads=[src_b], accum=[dst_b], sembuf=src_b, nonc=nonc)
            if final:
                P.outs.append(ev)
            return ev

        def dbg_dump(name, src_ap, shape, reads):
            if name not in debug_outs:
                return
            t = nc.dram_tensor(name, list(shape), BF16, kind="ExternalOutput").ap()
            db = P.buf(name)
            ev = P.dma("sp", "dma_start", out=t, in_=src_ap, reads=reads, writes=[db], sembuf=db)
            P.outs.append(ev)
            P.barrier()

        def load_w(src_ap, nk, ncols):
            t32, b32 = w32.get()
            tb_, bb = wbf.get()
            v32 = t32[:, 0:nk * ncols].rearrange("p (k n) -> p k n", k=nk)
            vb = tb_[:, 0:nk * ncols].rearrange("p (k n) -> p k n", k=nk)
            dma_in(v32, b32, src_ap)
            P.op("pool", "tensor_copy", vb, v32, reads=[b32], writes=[bb])
            return vb, bb

        def wview(w_ap, c0, ncols, nk=8, r0=0):
            return w_ap[r0:r0 + nk * 128, :].rearrange("(k p) n -> p k n", p=128)[:, :, c0:c0 + ncols]

        def mm_acc(bank, m0, M, n, pairs, reads):
            last = len(pairs) - 1
            for i, (l, r) in enumerate(pairs):
                P.op("pe", "matmul", ps[bank][0:M, 0:n], l, r, start=(i == 0), stop=(i == last),
                     reads=reads, writes=[bps[bank]], inc=(i == last))

        dma_in(cst[:], b_cst, cst_d[:, :])
        dma_in(vec[:], b_vec, vec_d[:, :])
        P.op("act", "copy", onesb[:], cst[:, 384:512], reads=[b_cst], writes=[b_onesb])
        P.op("act", "copy", identb[:], cst[:, 0:128], reads=[b_cst], writes=[b_identb])
        P.op("act", "mul", J8b[:], cst[:, 512:640], 8.0, reads=[b_cst], writes=[b_J8b])
        P.op("act", "activation", silc[:], V("cond", 0, 16), AF.Silu, reads=[b_vec], writes=[b_silc])
        for m in range(2):
            P.op("pool", "memset", rec[m][:], 0.0, writes=[b_rec[m]])

        def load_x(src, ntiles, tok0):
            for t in range(ntiles):
                s_t, s_b = stg.get()
                dma_in(s_t[:], s_b, src[t * 128:(t + 1) * 128, :])
                tok = tok0 + t * 128
                tb = tok // 512
                for half in range(2):
                    bk = nbank()
                    for c4 in range(4):
                        c = half * 4 + c4
                        P.op("pe", "transpose", ps[bk][:, c4 * 128:(c4 + 1) * 128], s_t[:, c * 128:(c + 1) * 128], ident,
                             reads=[s_b, b_cst], writes=[bps[bk]], inc=(c4 == 3))
                    eng = "dve" if half == 0 else "act"
                    dst = xT[:, half * 4:half * 4 + 4, tok:tok + 128]
                    srcv = ps[bk][:, 0:512].rearrange("p (c n) -> p c n", c=4)
                    if eng == "dve":
                        P.op("dve", "tensor_copy", dst, srcv, reads=[bps[bk]],
                             accum=[bx[half * 4 + i][tb] for i in range(4)])
                    else:
                        P.op("act", "copy", dst, srcv, reads=[bps[bk]],
                             accum=[bx[half * 4 + i][tb] for i in range(4)])

        load_x(xs, TS // 128, 0)
        load_x(xp, TP // 128, TS)

        def tr_store(src_t, src_b, col0, ncol, dst_ap, dst_b):
            bk = nbank()
            P.op("pe", "transpose", ps[bk][0:ncol, 0:128], src_t[:, col0:col0 + ncol], ident,
                 reads=[src_b, b_cst], writes=[bps[bk]])
            t_, b_ = bft.get()
            P.op("dve", "tensor_copy", t_[0:ncol, 0:128], ps[bk][0:ncol, 0:128], reads=[bps[bk]], writes=[b_])
            dma_out(dst_ap, dst_b, t_[0:ncol, 0:128], b_)

        b_CKM, b_CKN, b_CKG = P.buf("CKM"), P.buf("CKN"), P.buf("CKG")
        for tt in range(2):
            s_t, s_b = stg.get()
            dma_in(s_t[:, 0:256], s_b, c_ckv[tt * 128:(tt + 1) * 128, :])
            dma_in(s_t[:, 256:288], s_b, c_kr[tt * 128:(tt + 1) * 128, :], accum=True)
            for k in range(2):
                tr_store(s_t, s_b, k * 128, 128, CKM[k * 128:(k + 1) * 128, tt * 128:(tt + 1) * 128], b_CKM)
            tr_store(s_t, s_b, 256, 32, CKM[256:288, tt * 128:(tt + 1) * 128], b_CKM)
            s_t, s_b = stg.get()
            dma_in(s_t[:, 0:512], s_b, c_nk[tt * 128:(tt + 1) * 128, :])
            dma_in(s_t[:, 512:768], s_b, c_gk[tt * 128:(tt + 1) * 128, :], accum=True)
            for h in range(8):
                tr_store(s_t, s_b, h * 64, 64, CKN[h, :, tt * 128:(tt + 1) * 128], b_CKN)
            for g in range(2):
                tr_store(s_t, s_b, 512 + g * 128, 128, CKG[g, :, tt * 128:(tt + 1) * 128], b_CKG)

        def compute_mod(l):
            bk = nbank()
            for t in range(24):
                wv32, wb32 = w32.get()
                v32 = wv32[:, 0:2048].rearrange("p (k n) -> p k n", k=8)
                dma_in(v32, wb32, wmod[l].rearrange("(k p) n -> p k n", p=128)[:, :, t * 256:(t + 1) * 256])
                for cc in range(2):
                    ch = t * 2 + cc
                    for k in range(8):
                        P.op("pe", "matmul",
                            ps[bk][:, ch * 2:ch * 2 + 2], v32[:, k, cc * 128:(cc + 1) * 128],
                            silc[:, k * 2:k * 2 + 2], start=(k == 0), stop=(k == 7),
                            reads=[wb32, b_silc], writes=[bps[bk]], inc=(k == 7))
            pv = ps[bk][:, 0:96].rearrange("p (c two) -> p c two", two=2)
            bo, _ = VMAP[f"bmod{l}"]
            for cd in range(2):
                P.op("dve", "tensor_tensor", modv[:, cd, :], pv[:, :, cd], vec[:, bo:bo + 48], ALU.add,
                     reads=[bps[bk], b_vec], accum=[b_modv] if cd else (), writes=[] if cd else [b_modv])
            gm, gf = V(f"gmix{l}", 0, 8), V(f"gffn{l}", 0, 8)
            for cd in range(2):
                def s_(i, cd=cd):
                    return modv[:, cd, i * 8:(i + 1) * 8]
                first = (cd == 0)
                ops = [(0, s_(1), gm, True), (1, s_(0), None, False), (2, s_(2), None, False),
                       (3, s_(4), gf, True), (4, s_(3), None, False), (5, s_(5), None, False)]
                for j, (slot, src, g, isA) in enumerate(ops):
                    kw = dict(reads=[b_modv, b_vec])
                    if first and j == 0:
                        kw["writes"] = [b_dm]
                    else:
                        kw["accum"] = [b_dm]
                    if isA:
                        P.op("dve", "scalar_tensor_tensor",
                            dm[:, cd, slot, :], src, 1.0, g, ALU.add, ALU.mult, **kw)
                    else:
                        P.op("dve", "tensor_copy", dm[:, cd, slot, :], src, **kw)

        def rstd_block(src_fn, nchunks, tb, n, scale, reads_fn):
            bk = nbank()
            for c in range(nchunks):
                q_t, q_b = bft.get()
                P.op("act", "activation", q_t[:, 0:n], src_fn(c), AF.Square, reads=reads_fn(c), writes=[q_b])
                P.op("pe", "matmul", ps[bk][:, 0:n], onesb[:], q_t[:, 0:n], start=(c == 0), stop=(c == nchunks - 1),
                     reads=[q_b, b_onesb], writes=[bps[bk]], inc=(c == nchunks - 1))
            r_t, r_b = rsd.get()
            P.op("act", "activation", r_t[:, 0:n], ps[bk][:, 0:n], AF.Sqrt, bias=V("eps", 0, 1), scale=scale, reads=[bps[bk], b_vec], writes=[r_b])
            P.op("dve", "reciprocal", r_t[:, 0:n], r_t[:, 0:n], reads=[], writes=[r_b])
            return r_t, r_b

        def norm_mod(slotA, slotB):
            for tb in range(NB):
                cd = 0 if tb < 4 else 1
                cols = slice(tb * 512, (tb + 1) * 512)
                r_t, r_b = rstd_block(lambda c: xT[:, c, cols], 8, tb, 512, 1.0 / D, lambda c: [bx[c][tb]])
                for c in range(8):
                    t_t, t_b = f32t.get()
                    P.op("dve", "tensor_tensor", t_t[:], xT[:, c, cols], r_t[:], ALU.mult,
                         reads=[bx[c][tb], r_b], writes=[t_b])
                    P.op("act", "activation", hT[:, c, cols], t_t[:], AF.Identity,
                                                                      bias=dm[:, cd, slotB, c:c + 1], scale=dm[:, cd, slotA, c:c + 1],
                         reads=[t_b, b_dm], writes=[bh[c][tb]])

        def proj_h(wt, wb_, c0, M, tb, bank, src=None, srcb=None, nk=8):
            src = hT if src is None else src
            cols = slice(tb * 512, (tb + 1) * 512)
            pairs = [(wt[:, k, c0:c0 + M], src[:, k, cols]) for k in range(nk)]
            rb = [wb_] + ([bh[k][tb] for k in range(nk)] if srcb is None else srcb)
            mm_acc(bank, 0, M, 512, pairs, rb)

        def resid(bank, c, tb, cd, slotG):
            cols = slice(tb * 512, (tb + 1) * 512)
            P.op("dve", "scalar_tensor_tensor", xT[:, c, cols], ps[bank][:, 0:512], dm[:, cd, slotG, c:c + 1],
                                                        xT[:, c, cols], ALU.mult, ALU.add,
                 reads=[bps[bank], b_dm], writes=[bx[c][tb]])

        def out_proj(w_ap, slotG):
            for t in range(4):
                wt, wb_ = load_w(wview(w_ap, t * 256, 256), 8, 256)
                for cc in range(2):
                    c = t * 2 + cc
                    for tb in range(NB):
                        bk = nbank()
                        proj_h(wt, wb_, cc * 128, 128, tb, bk)
                        resid(bk, c, tb, 0 if tb < 4 else 1, slotG)

        def ffn(l):
            GRP = 4
            with ExitStack() as s2:
                hid = s2.enter_context(nc.sbuf_tensor(f"hid{l}", [128, GRP, NT], BF16))
                bhid = [[P.buf("hid") for _ in range(NB)] for _ in range(GRP)]
                for g0 in range(0, NFF, GRP):
                    gn = min(GRP, NFF - g0)
                    for jl in range(gn):
                        j = g0 + jl
                        t32, b32 = w32.get()
                        tb_, bb = wbf.get()
                        v32 = t32[:, 0:2048].rearrange("p (k n) -> p k n", k=8)
                        vb = tb_[:, 0:2048].rearrange("p (k n) -> p k n", k=8)
                        wsrc = wfi[l].rearrange("(k p) n -> p k n", p=128)
                        dma_in(v32[:, :, 0:128], b32, wsrc[:, :, j * 128:(j + 1) * 128])
                        dma_in(v32[:, :, 128:256], b32, wsrc[:, :, DFF + j * 128:DFF + (j + 1) * 128], accum=True)
                        P.op("pool", "tensor_copy", vb, v32, reads=[b32], writes=[bb])
                        for tb in range(NB):
                            bg, bu = nbank(), nbank()
                            proj_h(vb, bb, 0, 128, tb, bg)
                            proj_h(vb, bb, 128, 128, tb, bu)
                            s_t, s_b = f32t.get()
                            P.op("act", "activation", s_t[:], ps[bg][:, 0:512], AF.Silu, reads=[bps[bg]], writes=[s_b])
                            P.op("dve", "tensor_tensor",
                                hid[:, jl, tb * 512:(tb + 1) * 512], s_t[:], ps[bu][:, 0:512], ALU.mult,
                                reads=[s_b, bps[bu]], writes=[bhid[jl][tb]])
                    for t in range(4):
                        src = wfo[l][g0 * 128:(g0 + gn) * 128, :].rearrange("(k p) n -> p k n", p=128)[:, :, t * 256:(t + 1) * 256]
                        wt, wb_ = load_w(src, gn, 256)
                        for cc in range(2):
                            c = t * 2 + cc
                            for tb in range(NB):
                                bk = nbank()
                                proj_h(wt, wb_, cc * 128, 128, tb, bk, src=hid, srcb=[bhid[k][tb] for k in range(gn)], nk=gn)
                                resid(bk, c, tb, 0 if tb < 4 else 1, 5)
                flush_fin(); P.barrier()

        SB = (0, 1, 2)
        OB = (3, 4)
        UB = (5, 6)
        BB = 7
        actr = {"i": 0}
        pend = {"fin": None}

        def flush_fin():
            f_ = pend["fin"]
            pend["fin"] = None
            if f_ is not None:
                f_()

        def attention(q_ap, q_b, nq, dk, kchunks, vchunks, scale, mode, out_ap, out_bufs, echunks=None):
            it = actr["i"]
            actr["i"] += 1
            ob = OB[it % 2]
            ub = UB[it % 2]
            n = len(kchunks)
            ptile = [None] * n

            sbanks = SB if mode == 2 else (0, 1, 2, 5, 6)
            nsb = len(sbanks)

            def issue_s(i):
                bk = sbanks[i % nsb]
                kap, kb = kchunks[i]
                he = echunks is not None and echunks[i] is not None
                P.op("pe", "matmul", ps[bk][:, 0:nq], kap, q_ap, start=True, stop=(not he),
                     reads=[kb, q_b], writes=[bps[bk]])
                if he:
                    bap, bb_, map_, mb_ = echunks[i]
                    P.op("pe", "matmul", ps[bk][:, 0:nq], identb[:], map_, start=False, stop=False,
                         reads=[mb_, b_identb], writes=[bps[bk]])
                    P.op("pe", "matmul", ps[bk][:, 0:nq], J8b[:], bap, start=False, stop=True,
                         reads=[bb_, b_J8b], writes=[bps[bk]])

            for i0 in range(min(nsb - 1, n)):
                issue_s(i0)
            flush_fin()
            for i in range(n):
                bk = sbanks[i % nsb]
                p_t, p_b = bft.get()
                P.op("act", "activation", p_t[:, 0:nq], ps[bk][:, 0:nq], AF.Exp, scale=scale,
                     reads=[bps[bk]], writes=[p_b])
                if i + nsb - 1 < n:
                    issue_s(i + nsb - 1)
                vap, vb = vchunks[i]
                P.op("pe", "matmul", ps[ob][:, 0:nq], vap, p_t[:, 0:nq], start=(i == 0), stop=(i == n - 1),
                     reads=[vb, p_b], writes=[bps[ob]], inc=(mode != 2 and i == n - 1))
                if mode == 2:
                    P.op("pe", "matmul", ps[ub][:, 0:nq], onesb[:], p_t[:, 0:nq], start=(i == 0), stop=(i == n - 1),
                         reads=[p_b, b_onesb], writes=[bps[ub]], inc=(i == n - 1))
            pend["fin"] = lambda: fin_(mode, nq, ob, ub, out_ap, out_bufs)

        def fin_(mode, nq, ob, ub, out_ap, out_bufs):
            if mode == 2:
                r_t, r_b = f32t.get()
                P.op("dve", "reciprocal", r_t[:, 0:nq], ps[ub][:, 0:nq], reads=[bps[ub]], writes=[r_b])
                P.op("dve", "tensor_tensor", out_ap, ps[ob][:, 0:nq], r_t[:, 0:nq], ALU.mult,
                     reads=[bps[ob], r_b], accum=out_bufs)
            else:
                ro = slice(0, 64) if mode == 0 else slice(64, 128)
                rs = slice(64, 128) if mode == 0 else slice(0, 64)
                P.op("dve", "reciprocal", rec[mode][rs, 0:nq], ps[ob][rs, 0:nq], reads=[bps[ob]], writes=[b_rec[mode]])
                P.op("pe", "matmul", ps[BB][:, 0:nq], sel[mode], rec[mode][:, 0:nq], start=True, stop=True,
                     reads=[b_rec[mode], b_cst], writes=[bps[BB]])
                o_t, o_b = f32t.get()
                P.op("dve", "tensor_copy", o_t[ro, 0:nq], ps[ob][ro, 0:nq], reads=[bps[ob]], writes=[o_b])
                P.op("dve", "tensor_tensor", out_ap, o_t[ro, 0:nq], ps[BB][ro, 0:nq], ALU.mult,
                     reads=[o_b, bps[BB]], accum=out_bufs)

        def store_tok_major(src_t, src_b, rows, ntile, dst_fn, dst_b):
            for t in range(ntile):
                bk = nbank()
                P.op("pe", "transpose", ps[bk][:, 0:rows], src_t[0:rows, t * 128:(t + 1) * 128], ident[0:rows, 0:rows],
                     reads=[src_b, b_cst], writes=[bps[bk]])
                o_t, o_b = f32t.get()
                P.op("dve", "tensor_copy", o_t[:, 0:rows], ps[bk][:, 0:rows], reads=[bps[bk]], writes=[o_b])
                dma_out(dst_fn(t), dst_b, o_t[:, 0:rows], o_b, final=True, nonc=True)

        def layer0():
            compute_mod(0)
            norm_mod(0, 1)
            b = {n: P.buf(n) for n in ("QM", "KVC", "KVCa", "CKP", "QN", "KN", "KNP", "KNS", "KNSa", "VN", "VNP", "VNS", "VNSa", "st")}
            with ExitStack() as s2:
                c32 = Pool_("c32", (128, 512), F32, 4, s2)
                nT = Pool_("nT", (128, 512), BF16, 4, s2)
                rp = s2.enter_context(nc.sbuf_tensor("rp", [128, 2, 512], F32))
                b_rp = P.buf("rp")
                wq_t = s2.enter_context(nc.sbuf_tensor("wq_t", [128, 2, 1536], BF16))
                b_wq = P.buf("wq")
                wts = [load_w(wview(wa, WA_CQ, 256), 8, 256), load_w(wview(wa, WA_CKV, 256), 8, 256)]
                P.dma("pool", "dma_start", out=wq_t[:], in_=wuq.rearrange("(k p) n -> p k n", p=128),
                      writes=[b_wq], sembuf=b_wq)
                for tb in range(NB):
                    cols = slice(tb * 512, (tb + 1) * 512)
                    for which in range(2):
                        wt, wb_ = wts[which]
                        gname = "gq" if which == 0 else "gkv"
                        n32 = [None, None]
                        n32b = [None, None]
                        for cc in range(2):
                            bk = nbank()
                            proj_h(wt, wb_, cc * 128, 128, tb, bk)
                            c_t, c_b = c32.get()
                            P.op("act", "copy", c_t[:], ps[bk][:, 0:512], reads=[bps[bk]], writes=[c_b])
                            n32[cc], n32b[cc] = c_t, c_b
                        r_t, r_b = rstd_block(lambda c: n32[c][:], 2, tb, 512, 1.0 / 256, lambda c: [n32b[c]])
                        n_t = [None, None]
                        n_b = [None, None]
                        for c in range(2):
                            P.op("dve", "scalar_tensor_tensor", n32[c][:], n32[c][:], V(gname, c, 1), r_t[:], ALU.mult, ALU.mult,
                                 reads=[r_b, b_vec], writes=[n32b[c]])
                            nb_t, nb_b = nT.get()
                            P.op("act", "copy", nb_t[:], n32[c][:], reads=[n32b[c]], writes=[nb_b])
                            n_t[c], n_b[c] = nb_t, nb_b
                        if which == 1:
                            for c in range(2):
                                if tb < 4:
                                    dma_out(KVC[c * 128:(c + 1) * 128, cols], b["KVC"], n_t[c][:], n_b[c])
                                else:
                                    dma_out(CKP[c * 128:(c + 1) * 128, :], b["CKP"], n_t[c][:], n_b[c])
                                    store_tok_major(n32[c], n32b[c], 128, 4,
                                                    lambda t, c=c: st_ckv[t * 128:(t + 1) * 128, c * 128:(c + 1) * 128], b["st"])
                        else:
                            if tb < 4:
                                dma_in(rp[64:96, :, :], b_rp, ropem[:, :, cols])
                            for h in range(8):
                                b1 = nbank()
                                mm_acc(b1, 0, 96, 512, [(wq_t[:, k, h * 96:(h + 1) * 96], n_t[k][:]) for k in range(2)], [b_wq, n_b[0], n_b[1]])
                                q_t, q_b = bft.get()
                                P.op("act", "copy", q_t[0:64, :], ps[b1][0:64, 0:512], reads=[bps[b1]], writes=[q_b])
                                if tb < 4:
                                    b2 = nbank()
                                    mm_acc(b2, 0, 96, 512, [(wq_t[:, k, 768 + h * 96:768 + (h + 1) * 96], n_t[k][:]) for k in range(2)], [b_wq, n_b[0], n_b[1]])
                                    u_t, u_b = f32t.get()
                                    v_t, v_b = f32t.get()
                                    P.op("dve", "tensor_tensor", u_t[64:96, :], ps[b1][64:96, 0:512], rp[64:96, 0, :], ALU.mult,
                                         reads=[bps[b1], b_rp], writes=[u_b])
                                    P.op("dve", "tensor_tensor", v_t[64:96, :], ps[b2][64:96, 0:512], rp[64:96, 1, :], ALU.mult,
                                         reads=[bps[b2], b_rp], writes=[v_b])
                                    P.op("dve", "tensor_tensor", q_t[64:96, :], u_t[64:96, :], v_t[64:96, :], ALU.add,
                                         reads=[u_b, v_b], accum=[q_b])
                                else:
                                    P.op("dve", "tensor_copy", q_t[64:96, :], ps[b1][64:96, 0:512], reads=[bps[b1]], accum=[q_b])
                                dma_out(QM[h, :, cols], b["QM"], q_t[0:96, :], q_b)
                wt, wb_ = load_w(wview(wa, WA_KR, 64), 8, 64)
                for tb in range(NB):
                    cols = slice(tb * 512, (tb + 1) * 512)
                    b1 = nbank()
                    proj_h(wt, wb_, 0, 32, tb, b1)
                    k_t, k_b = bft.get()
                    if tb < 4:
                        b2 = nbank()
                        proj_h(wt, wb_, 32, 32, tb, b2)
                        dma_in(rp[0:32, :, :], b_rp, ropem[:, :, cols])
                        u_t, u_b = f32t.get()
                        v_t, v_b = f32t.get()
                        P.op("dve", "tensor_tensor", u_t[0:32, :], ps[b1][0:32, 0:512], rp[0:32, 0, :], ALU.mult,
                             reads=[bps[b1], b_rp], writes=[u_b])
                        P.op("dve", "tensor_tensor", v_t[0:32, :], ps[b2][0:32, 0:512], rp[0:32, 1, :], ALU.mult,
                             reads=[bps[b2], b_rp], writes=[v_b])
                        P.op("dve", "tensor_tensor", k_t[0:32, :], u_t[0:32, :], v_t[0:32, :], ALU.add,
                             reads=[u_b, v_b], writes=[k_b])
                        dma_out(KVC[256:288, cols], b["KVC"], k_t[0:32, :], k_b)
                    else:
                        u_t, u_b = f32t.get()
                        P.op("act", "copy", u_t[0:32, :], ps[b1][0:32, 0:512], reads=[bps[b1]], writes=[u_b])
                        P.op("dve", "tensor_copy", k_t[0:32, :], u_t[0:32, :], reads=[u_b], writes=[k_b])
                        dma_out(CKP[256:288, :], b["CKP"], k_t[0:32, :], k_b)
                        store_tok_major(u_t, u_b, 32, 4, lambda t: st_kr[t * 128:(t + 1) * 128, :], b["st"])
                flush_fin(); P.barrier()
            P.dma("pool", "collective_compute", "AllGather", ALU.bypass, replica_groups=PAIRS,
                                                         ins=[KVC[:, :].opt()], outs=[KVCa[:, :].opt()],
                  reads=[b["KVC"]], writes=[b["KVCa"]], sembuf=b["KVCa"], inc=1)
            for which in range(2):
                for t in range(2):
                    wt, wb_ = load_w(wview(wa, (WA_NQ if which == 0 else WA_NK) + t * 256, 256), 8, 256)
                    for cc in range(2):
                        hp = t * 2 + cc
                        for tb in range(NB):
                            cols = slice(tb * 512, (tb + 1) * 512)
                            bk = nbank()
                            proj_h(wt, wb_, cc * 128, 128, tb, bk)
                            o_t, o_b = bft.get()
                            if which == 1 and tb == 4:
                                f_t, f_b = f32t.get()
                                P.op("act", "copy", f_t[:], ps[bk][:, 0:512], reads=[bps[bk]], writes=[f_b])
                                P.op("dve", "tensor_copy", o_t[:], f_t[:], reads=[f_b], writes=[o_b])
                                store_tok_major(f_t, f_b, 128, 4, lambda t_, hp=hp: st_nk[t_ * 128:(t_ + 1) * 128, hp * 128:(hp + 1) * 128], b["st"])
                            else:
                                P.op("act", "copy", o_t[:], ps[bk][:, 0:512], reads=[bps[bk]], writes=[o_b])
                            for hh in range(2):
                                h = hp * 2 + hh
                                rows = slice(hh * 64, hh * 64 + 64)
                                if which == 0:
                                    dma_out(QN[h, :, cols], b["QN"], o_t[rows, :], o_b)
                                elif tb < 4:
                                    dma_out(KN[h, :, cols], b["KN"], o_t[rows, :], o_b)
                                    if tb == 3:
                                        for rr_ in range(4):
                                            dma_out(KNS[h * 64:(h + 1) * 64, rr_ * 64:(rr_ + 1) * 64], b["KNS"],
                                                    o_t[rows, (7 - rr_) * 64:(8 - rr_) * 64], o_b)
                                else:
                                    dma_out(KNP[h, :, :], b["KNP"], o_t[rows, :], o_b)
            for t in range(2):
                wt, wb_ = load_w(wview(wa, WA_NV + t * 256, 256), 8, 256)
                for tt in range(NT // 128):
                    bk = nbank()
                    tb = tt // 4
                    mm_acc(bk, 0, 128, 256, [(hT[:, k, tt * 128:(tt + 1) * 128], wt[:, k, :]) for k in range(8)],
                           [wb_] + [bh[k][tb] for k in range(8)])
                    o_t, o_b = bft.get()
                    if tb == 4:
                        f_t, f_b = f32t.get()
                        P.op("act", "copy", f_t[:, 0:256], ps[bk][:, 0:256], reads=[bps[bk]], writes=[f_b])
                        P.op("dve", "tensor_copy", o_t[:, 0:256], f_t[:, 0:256], reads=[f_b], writes=[o_b])
                        pt = tt - TS // 128
                        dma_out(st_nv[pt * 128:(pt + 1) * 128, t * 256:(t + 1) * 256], b["st"], f_t[:, 0:256], f_b, final=True, nonc=True)
                        dma_out(VNP[pt * 128:(pt + 1) * 128, t * 256:(t + 1) * 256], b["VNP"], o_t[:, 0:256], o_b, nonc=True)
                    else:
                        P.op("act", "copy", o_t[:, 0:256], ps[bk][:, 0:256], reads=[bps[bk]], writes=[o_b])
                        dma_out(VN[tt * 128:(tt + 1) * 128, t * 256:(t + 1) * 256], b["VN"], o_t[:, 0:256], o_b, nonc=True)
                        if tt >= 14:
                            for hr in range(2):
                                lrow = (tt - 14) * 2 + hr
                                dpos = 3 - lrow
                                dma_out(VNS[dpos * 64:(dpos + 1) * 64, t * 256:(t + 1) * 256], b["VNS"],
                                        o_t[hr * 64:(hr + 1) * 64, 0:256], o_b, nonc=True)
            P.dma("pool", "collective_compute", "AllGather", ALU.bypass, replica_groups=PAIRS,
                                                         ins=[KNS[:, :].opt()], outs=[KNSa[:, :].opt()],
                  reads=[b["KNS"]], writes=[b["KNSa"]], sembuf=b["KNSa"], inc=1)
            P.dma("pool", "collective_compute", "AllGather", ALU.bypass, replica_groups=PAIRS,
                                                         ins=[VNS[:, :].opt()], outs=[VNSa[:, :].opt()],
                  reads=[b["VNS"]], writes=[b["VNSa"]], sembuf=b["VNSa"], inc=1)
            flush_fin(); P.barrier()
            scopeA.close()
            with ExitStack() as s2:
                KT = s2.enter_context(nc.sbuf_tensor("naK", [64, 3328], BF16))
                b_KT = P.buf("naK")
                VA = [s2.enter_context(nc.sbuf_tensor(f"naV{i}", [128, 26, 128], BF16)) for i in range(2)]
                b_VA = [P.buf("naV0"), P.buf("naV1")]
                QT = s2.enter_context(nc.sbuf_tensor("naQ", [64, NT], BF16))
                b_QT = P.buf("naQ")
                MK = s2.enter_context(nc.sbuf_tensor("naM", [128, 18, 512], BF16))
                b_MK = P.buf("naM")
                P.dma("pool", "dma_start", out=MK[:], in_=masks.rearrange("t p q -> p t q"), writes=[b_MK], sembuf=b_MK)
                Tz32 = s2.enter_context(nc.sbuf_tensor("naTz32", [128, RR * 64], F32))
                b_Tz32 = P.buf("naTz32")
                Tzb = [s2.enter_context(nc.sbuf_tensor(f"naTzb{i}", [128, RR * 64], BF16)) for i in range(2)]
                b_Tzb = [P.buf("naTzb0"), P.buf("naTzb1")]
                P.op("pool", "memset", VA[0][:, :, 64:128], 1.0, writes=[b_VA[0]])
                P.op("pool", "memset", VA[1][:, :, 0:64], 1.0, writes=[b_VA[1]])
                for h in range(8):
                    m = h % 2
                    vc0 = 0 if m == 0 else 64
                    tzv = Tz32[:, :].rearrange("p (r q) -> p r q", q=64)
                    dma_nc(tzv[0:64, :, :], b_Tz32, bass.AP(rpbp.tensor, h * R2 * RW + RW + 16, [[1, 64], [RW, RR], [1, 64]]))
                    dma_nc(tzv[64:128, :, :], b_Tz32, bass.AP(rpbp.tensor, h * R2 * RW + 16, [[1, 64], [RW, RR], [1, 64]]), accum=True)
                    P.op("act", "copy", Tzb[m][:, :], Tz32[:, :], reads=[b_Tz32], writes=[b_Tzb[m]])
                    dma_in(KT[:, 0:2048], b_KT, KN[h, :, :], reads=[b["KN"]])
                    dma_in(KT[:, 2048:2304], b_KT, KNSa[h * 64:(h + 1) * 64, :], reads=[b["KNSa"]], accum=True)
                    dma_in(KT[:, 2304:2560], b_KT, KNSa[512 + h * 64:512 + (h + 1) * 64, :], reads=[b["KNSa"]], accum=True)
                    dma_in(KT[:, 2560:2816], b_KT, CKN[h, :, :], reads=[b_CKN], accum=True)
                    dma_in(KT[:, 2816:3328], b_KT, KNP[h, :, :], reads=[b["KNP"]], accum=True)
                    dma_in(QT[:, :], b_QT, QN[h, :, :], reads=[b["QN"]])
                    hs = slice(h * 64, (h + 1) * 64)
                    dma_nc(VA[m][:, 0:16, vc0:vc0 + 64], b_VA[m], VN[:, hs].rearrange("(c p) d -> p c d", p=128), reads=[b["VN"]], accum=True)
                    dma_nc(VA[m][:, 16:18, vc0:vc0 + 64], b_VA[m], VNSa[0:256, hs].rearrange("(c p) d -> p c d", p=128), reads=[b["VNSa"]], accum=True)
                    dma_nc(VA[m][:, 18:20, vc0:vc0 + 64], b_VA[m], VNSa[256:512, hs].rearrange("(c p) d -> p c d", p=128), reads=[b["VNSa"]], accum=True)
                    dma_nc(VA[m][:, 20:22, vc0:vc0 + 64], b_VA[m], c_nv[:, hs].rearrange("(c p) d -> p c d", p=128), q="pool", accum=True)
                    dma_nc(VA[m][:, 22:26, vc0:vc0 + 64], b_VA[m], VNP[:, hs].rearrange("(c p) d -> p c d", p=128), reads=[b["VNP"]], accum=True)

                    def ec(ti, m=m):
                        if ti < 6:
                            dr0 = 2 * ti + 7
                        elif ti < 14:
                            dr0 = 2 * (ti - 6) + 3
                        else:
                            dr0 = 2 * (6 + (ti - 14) % 2) + 3
                        r0_ = (RR - 1) - (dr0 + 4)
                        return (Tzb[m][:, r0_ * 64:(r0_ + 8) * 64], b_Tzb[m], MK[:, ti, :], b_MK)
                    kc = lambda c: (KT[:, c * 128:(c + 1) * 128], b_KT)
                    vc = lambda c: (VA[m][:, c, :], b_VA[m])
                    for j in range(4):
                        if j == 0:
                            own, es = list(range(0, 6)), list(range(0, 6))
                        elif j < 3:
                            own, es = list(range(4 * j - 2, 4 * j + 6)), list(range(6, 14))
                        else:
                            own, es = list(range(10, 20)), list(range(6, 12)) + [14, 15, 16, 17]
                        cks = own + [20, 21]
                        ecs = [ec(t_) for t_ in es] + [None, None]
                        attention(QT[:, j * 512:(j + 1) * 512], b_QT, 512, 64, [kc(c) for c in cks], [vc(c) for c in cks],
                                  0.125, m, hT[m * 64:(m + 1) * 64, 4 + h // 2, j * 512:(j + 1) * 512], [bh[4 + h // 2][j]], echunks=ecs)
                    for pb in range(2):
                        cks = [22 + 2 * pb, 23 + 2 * pb]
                        q0 = TS + pb * 256
                        attention(QT[:, q0:q0 + 256], b_QT, 256, 64, [kc(c) for c in cks], [vc(c) for c in cks],
                                  0.125, m, hT[m * 64:(m + 1) * 64, 4 + h // 2, q0:q0 + 256], [bh[4 + h // 2][4]])
            flush_fin(); P.barrier()
            with ExitStack() as s2:
                CA = s2.enter_context(nc.sbuf_tensor("mlC", [128, 2, 4864], BF16))
                b_CA = P.buf("mlC")
                KT = s2.enter_context(nc.sbuf_tensor("mlK", [96, 4864], BF16))
                b_KT = P.buf("mlK")
                VA = [s2.enter_context(nc.sbuf_tensor(f"mlV{i}", [128, 38, 128], BF16)) for i in range(2)]
                b_VA = [P.buf("mlV0"), P.buf("mlV1")]
                QT = s2.enter_context(nc.sbuf_tensor("mlQ", [96, NT], BF16))
                b_QT = P.buf("mlQ")
                wk_t = s2.enter_context(nc.sbuf_tensor("wukv_t", [128, 2, 1024], BF16))
                b_wk = P.buf("wukv")
                P.dma("pool", "dma_start", out=wk_t[:], in_=wukv.rearrange("(k p) n -> p k n", p=128), writes=[b_wk], sembuf=b_wk)
                P.op("pool", "memset", VA[0][:, :, 64:128], 1.0, writes=[b_VA[0]])
                P.op("pool", "memset", VA[1][:, :, 0:64], 1.0, writes=[b_VA[1]])
                for r in range(2):
                    dma_in(CA[:, :, r * TS:(r + 1) * TS], b_CA, KVCa[r * 288:r * 288 + 256, :].rearrange("(k p) n -> p k n", p=128),
                           reads=[b["KVCa"]], accum=(r > 0))
                dma_in(CA[:, :, 4096:4352], b_CA, CKM[0:256, :].rearrange("(k p) n -> p k n", p=128), reads=[b_CKM], accum=True)
                dma_in(CA[:, :, 4352:4864], b_CA, CKP[0:256, :].rearrange("(k p) n -> p k n", p=128), reads=[b["CKP"]], accum=True)
                for h in range(8):
                    m = h % 2
                    vc0 = 0 if m == 0 else 64
                    dma_in(KT[64:96, 0:2048], b_KT, KVCa[256:288, :], reads=[b["KVCa"]])
                    dma_in(KT[64:96, 2048:4096], b_KT, KVCa[544:576, :], reads=[b["KVCa"]], accum=True)
                    dma_in(KT[64:96, 4096:4352], b_KT, CKM[256:288, :], reads=[b_CKM], accum=True)
                    dma_in(KT[64:96, 4352:4864], b_KT, CKP[256:288, :], reads=[b["CKP"]], accum=True)
                    dma_in(QT[:, :], b_QT, QM[h, :, :], reads=[b["QM"]])
                    for cb in range(10):
                        c0 = cb * 512
                        n = min(512, 4864 - c0)
                        bk = nbank()
                        mm_acc(bk, 0, 64, n, [(wk_t[:, k, h * 64:(h + 1) * 64], CA[:, k, c0:c0 + n]) for k in range(2)], [b_wk, b_CA])
                        P.op("act", "copy", KT[0:64, c0:c0 + n], ps[bk][0:64, 0:n], reads=[bps[bk]], accum=[b_KT])
                    for g8 in range(0, 38, 8):
                        ng = min(8, 38 - g8)
                        bk = nbank()
                        for ci in range(ng):
                            ck = g8 + ci
                            for k in range(2):
                                P.op("pe", "matmul",
                                    ps[bk][:, ci * 64:(ci + 1) * 64], CA[:, k, ck * 128:(ck + 1) * 128],
                                    wk_t[:, k, 512 + h * 64:512 + (h + 1) * 64], start=(k == 0), stop=(k == 1),
                                    reads=[b_wk, b_CA], writes=[bps[bk]], inc=(ci == ng - 1 and k == 1))
                        P.op("dve", "tensor_copy",
                            VA[m][:, g8:g8 + ng, vc0:vc0 + 64], ps[bk][:, 0:ng * 64].rearrange("p (c d) -> p c d", d=64),
                            reads=[bps[bk]], accum=[b_VA[m]])
                    kc = lambda c: (KT[:, c * 128:(c + 1) * 128], b_KT)
                    vc = lambda c: (VA[m][:, c, :], b_VA[m])
                    sc = 96.0 ** -0.5
                    for j in range(4):
                        cks = list(range(34))
                        attention(QT[:, j * 512:(j + 1) * 512], b_QT, 512, 96, [kc(c) for c in cks], [vc(c) for c in cks],
                                  sc, m, hT[m * 64:(m + 1) * 64, h // 2, j * 512:(j + 1) * 512], [bh[h // 2][j]])
                    for pb in range(2):
                        cks = [34 + 2 * pb, 35 + 2 * pb]
                        q0 = TS + pb * 256
                        attention(QT[:, q0:q0 + 256], b_QT, 256, 96, [kc(c) for c in cks], [vc(c) for c in cks],
                                  sc, m, hT[m * 64:(m + 1) * 64, h // 2, q0:q0 + 256], [bh[h // 2][4]])
                    flush_fin()
            flush_fin(); P.barrier()
            dbg_dump("AT0", hT[:], (128, 8, NT), [bh[c][t] for c in range(8) for t in range(NB)])
            open_pools(stack)
            out_proj(woa, 2)
            norm_mod(3, 4)
            ffn(0)

        def layer1():
            compute_mod(1)
            norm_mod(0, 1)
            b = {n: P.buf(n) for n in ("QG", "KG", "KGa", "KGP", "VG", "VGa", "VGP", "st")}
            with ExitStack() as s2:
                rg = s2.enter_context(nc.sbuf_tensor("rg", [128, 2, 512], F32))
                b_rg = P.buf("rg")
                tq = s2.enter_context(nc.sbuf_tensor("tq", [128, 4, 512], F32))
                b_tq = P.buf("tq")
                for isk in range(2):
                    nh = 2 if isk else 8
                    for hh in range(0, nh, 2):
                        c_main = (WC_K if isk else WC_Q) + hh * 128
                        c_sw = (WC_KS if isk else WC_QS) + hh * 128
                        wm, wmb = load_w(wview(wc, c_main, 256), 8, 256)
                        wsw, wswb = load_w(wview(wc, c_sw, 256), 8, 256)
                        for tb in range(NB):
                            cols = slice(tb * 512, (tb + 1) * 512)
                            if tb < 4:
                                dma_in(rg[:], b_rg, ropeg[:, :, cols])
                                g_, gs_ = ("ggk", "ggks") if isk else ("ggq", "ggqs")
                                P.op("dve", "tensor_scalar", tq[:, 2 * isk, :], rg[:, 0, :], V(g_, 0, 1), None, ALU.mult,
                                     reads=[b_rg, b_vec], writes=[b_tq])
                                P.op("dve", "tensor_scalar", tq[:, 2 * isk + 1, :], rg[:, 1, :], V(gs_, 0, 1), None, ALU.mult,
                                     reads=[b_rg, b_vec], accum=[b_tq])
                            for h2 in range(2):
                                h = hh + h2
                                b1 = nbank()
                                proj_h(wm, wmb, h2 * 128, 128, tb, b1)
                                r_t, r_b = rstd_block(lambda c, b1=b1: ps[b1][:, 0:512], 1, tb, 512, 1.0 / 128, lambda c, b1=b1: [bps[b1]])
                                o_t, o_b = bft.get()
                                if tb < 4:
                                    b2 = nbank()
                                    proj_h(wsw, wswb, h2 * 128, 128, tb, b2)
                                    u_t, u_b = f32t.get()
                                    P.op("dve", "tensor_tensor", u_t[:], ps[b1][:, 0:512], tq[:, 2 * isk, :], ALU.mult,
                                         reads=[bps[b1], b_tq], writes=[u_b])
                                    v_t, v_b = f32t.get()
                                    P.op("dve", "tensor_tensor", v_t[:], ps[b2][:, 0:512], tq[:, 2 * isk + 1, :], ALU.mult,
                                         reads=[bps[b2], b_tq], writes=[v_b])
                                    P.op("pool", "tensor_tensor", u_t[:], u_t[:], v_t[:], ALU.add,
                                         reads=[v_b], writes=[u_b])
                                    P.op("dve", "tensor_tensor", o_t[:], u_t[:], r_t[:], ALU.mult,
                                         reads=[u_b, r_b], writes=[o_b])
                                    if isk:
                                        dma_out(KG[h * 128:(h + 1) * 128, cols], b["KG"], o_t[:], o_b)
                                    else:
                                        dma_out(QG[h, :, cols], b["QG"], o_t[:], o_b)
                                else:
                                    g_ = "ggk" if isk else "ggq"
                                    u_t, u_b = f32t.get()
                                    P.op("dve", "scalar_tensor_tensor",
                                        u_t[:], ps[b1][:, 0:512], V(g_, 0, 1), r_t[:], ALU.mult, ALU.mult, reads=[bps[b1], r_b, b_vec], writes=[u_b])
                                    P.op("act", "copy", o_t[:], u_t[:], reads=[u_b], writes=[o_b])
                                    if isk:
                                        dma_out(KGP[h, :, :], b["KGP"], o_t[:], o_b)
                                        store_tok_major(u_t, u_b, 128, 4, lambda t, h=h: st_gk[t * 128:(t + 1) * 128, h * 128:(h + 1) * 128], b["st"])
                                    else:
                                        dma_out(QG[h, :, cols], b["QG"], o_t[:], o_b)
                wt, wb_ = load_w(wview(wc, WC_V, 256), 8, 256)
                for tt in range(NT // 128):
                    bk = nbank()
                    tb = tt // 4
                    mm_acc(bk, 0, 128, 256, [(hT[:, k, tt * 128:(tt + 1) * 128], wt[:, k, :]) for k in range(8)],
                           [wb_] + [bh[k][tb] for k in range(8)])
                    o_t, o_b = bft.get()
                    if tb == 4:
                        f_t, f_b = f32t.get()
                        P.op("act", "copy", f_t[:, 0:256], ps[bk][:, 0:256], reads=[bps[bk]], writes=[f_b])
                        P.op("dve", "tensor_copy", o_t[:, 0:256], f_t[:, 0:256], reads=[f_b], writes=[o_b])
                        pt = tt - TS // 128
                        dma_out(st_gv[pt * 128:(pt + 1) * 128, :], b["st"], f_t[:, 0:256], f_b, final=True)
                        dma_out(VGP[pt * 128:(pt + 1) * 128, :], b["VGP"], o_t[:, 0:256], o_b)
                    else:
                        P.op("act", "copy", o_t[:, 0:256], ps[bk][:, 0:256], reads=[bps[bk]], writes=[o_b])
                        dma_out(VG[tt * 128:(tt + 1) * 128, :], b["VG"], o_t[:, 0:256], o_b)
                flush_fin(); P.barrier()
            P.dma("pool", "collective_compute", "AllGather", ALU.bypass, replica_groups=PAIRS,
                                                         ins=[KG[:, :].opt()], outs=[KGa[:, :].opt()],
                  reads=[b["KG"]], writes=[b["KGa"]], sembuf=b["KGa"], inc=1)
            P.dma("pool", "collective_compute", "AllGather", ALU.bypass, replica_groups=PAIRS,
                                                         ins=[VG[:, :].opt()], outs=[VGa[:, :].opt()],
                  reads=[b["VG"]], writes=[b["VGa"]], sembuf=b["VGa"], inc=1)
            with ExitStack() as s2:
                KT = s2.enter_context(nc.sbuf_tensor("gqK", [128, 4864], BF16))
                b_KT = P.buf("gqK")
                VT = s2.enter_context(nc.sbuf_tensor("gqV", [128, 38, 128], BF16))
                b_VT = P.buf("gqV")
                QTs = [s2.enter_context(nc.sbuf_tensor(f"gqQ{i}", [128, NT], BF16)) for i in range(2)]
                b_QTs = [P.buf("gqQ0"), P.buf("gqQ1")]
                for h in range(8):
                    g = h // 4
                    if h % 4 == 0:
                        for r in range(2):
                            dma_in(KT[:, r * TS:(r + 1) * TS], b_KT, KGa[r * 256 + g * 128:r * 256 + (g + 1) * 128, :], reads=[b["KGa"]], accum=(r > 0))
                        dma_in(KT[:, 4096:4352], b_KT, CKG[g, :, :], reads=[b_CKG], accum=True)
                        dma_in(KT[:, 4352:4864], b_KT, KGP[g, :, :], reads=[b["KGP"]], accum=True)
                        gs = slice(g * 128, (g + 1) * 128)
                        dma_nc(VT[:, 0:32, :], b_VT, VGa[:, gs].rearrange("(c p) d -> p c d", p=128), reads=[b["VGa"]])
                        dma_nc(VT[:, 32:34, :], b_VT, c_gv[:, gs].rearrange("(c p) d -> p c d", p=128), q="pool", accum=True)
                        dma_nc(VT[:, 34:38, :], b_VT, VGP[:, gs].rearrange("(c p) d -> p c d", p=128), reads=[b["VGP"]], accum=True)
                    QT, b_QT = QTs[h % 2], b_QTs[h % 2]
                    dma_in(QT[:, :], b_QT, QG[h, :, :], reads=[b["QG"]])
                    kc = lambda c: (KT[:, c * 128:(c + 1) * 128], b_KT)
                    vc = lambda c: (VT[:, c, :], b_VT)
                    sc = 128.0 ** -0.5
                    for j in range(4):
                        cks = list(range(34))
                        attention(QT[:, j * 512:(j + 1) * 512], b_QT, 512, 128, [kc(c) for c in cks], [vc(c) for c in cks],
                                  sc, 2, hT[:, h, j * 512:(j + 1) * 512], [bh[h][j]])
                    for pb in range(2):
                        cks = [34 + 2 * pb, 35 + 2 * pb]
                        q0 = TS + pb * 256
                        attention(QT[:, q0:q0 + 256], b_QT, 256, 128, [kc(c) for c in cks], [vc(c) for c in cks],
                                  sc, 2, hT[:, h, q0:q0 + 256], [bh[h][4]])
            flush_fin(); P.barrier()
            dbg_dump("AT1", hT[:], (128, 8, NT), [bh[c][t] for c in range(8) for t in range(NB)])
            out_proj(woc, 2)
            norm_mod(3, 4)
            ffn(1)

        layer0()
        layer1()

        b_y = P.buf("y")
        go, _ = VMAP["gfin"]
        for tb in range(NB):
            cols = slice(tb * 512, (tb + 1) * 512)
            r_t, r_b = rstd_block(lambda c: xT[:, c, cols], 8, tb, 512, 1.0 / D, lambda c: [bx[c][tb]])
            for c in range(8):
                P.op("dve", "scalar_tensor_tensor", xT[:, c, cols], xT[:, c, cols], vec[:, go + c:go + c + 1], r_t[:], ALU.mult, ALU.mult,
                     reads=[r_b, b_vec], writes=[bx[c][tb]])
            for t4 in range(4):
                tok = tb * 512 + t4 * 128
                s_t, s_b = stg.get()
                for half in range(2):
                    bk = nbank()
                    for c4 in range(4):
                        c = half * 4 + c4
                        P.op("pe", "transpose", ps[bk][:, c4 * 128:(c4 + 1) * 128], xT[:, c, tok:tok + 128], ident,
                             reads=[bx[c][tb], b_cst], writes=[bps[bk]], inc=(c4 == 3))
                    if half == 0:
                        P.op("dve", "tensor_copy", s_t[:, 0:512], ps[bk][:, 0:512], reads=[bps[bk]], writes=[s_b])
                    else:
                        P.op("act", "copy", s_t[:, 512:1024], ps[bk][:, 0:512], reads=[bps[bk]], accum=[s_b])
                if tok < TS:
                    dma_out(y_s[tok:tok + 128, :], b_y, s_t[:], s_b, final=True)
                else:
                    dma_out(y_p[tok - TS:tok - TS + 128, :], b_y, s_t[:], s_b, final=True)

        P.emit(nc, stack)
    return nc


def _rope_tables(tok_rows, tok_cols, rot_dim):
    axis_dim = rot_dim // 2
    inv = (10000.0 ** (-np.arange(0, axis_dim, 2, dtype=np.float32) / axis_dim)).astype(np.float32)
    ang = np.concatenate([tok_rows[:, None].astype(np.float32) * inv, tok_cols[:, None].astype(np.float32) * inv], axis=-1)
    c, s = np.cos(ang).astype(np.float32), np.sin(ang).astype(np.float32)
    cos2 = np.concatenate([c, c], axis=1).T
    sins = np.concatenate([-s, s], axis=1).T
    return np.ascontiguousarray(np.stack([cos2, sins], axis=1)).astype(np.float32)


def _na_masks(half):
    m = np.full((18, 128, 512), 8.0 * MASKNEG, np.float32)
    qc = np.arange(64)
    cs = np.clip(qc - 8, 0, 48)
    kcol = np.arange(64)
    colok = (kcol[:, None] >= cs[None, :]) & (kcol[:, None] < cs[None, :] + 16)

    def glob(l):
        return l if half == 0 else 63 - l

    def tile(ti, krows, qrows, partner_rank=None):
        for a, kl in enumerate(krows):
            for i, ql in enumerate(qrows):
                if kl is None or kl < 0 or kl > 35:
                    continue
                if kl >= 32:
                    pl = 31 - (kl - 32)
                    kg = (63 - pl) if half == 0 else pl
                    if partner_rank is not None and partner_rank != (1 - half):
                        continue
                else:
                    kg = glob(kl)
                    if partner_rank is not None:
                        continue
                qg = glob(ql)
                rs = min(max(qg - 4, 0), 56)
                if rs <= kg < rs + 8:
                    blk = m[ti, a * 64:(a + 1) * 64, i * 64:(i + 1) * 64]
                    blk[colok] = 0.0
    for c in range(6):
        tile(c, [2 * c, 2 * c + 1], list(range(8)))
    for c in range(8):
        j = 1
        tile(6 + c, [8 * j - 4 + 2 * c, 8 * j - 4 + 2 * c + 1], [8 * j + i for i in range(8)])
    for e in range(2):
        tile(14 + e, [32 + 2 * e, 33 + 2 * e], [24 + i for i in range(8)], partner_rank=0)
        tile(16 + e, [32 + 2 * e, 33 + 2 * e], [24 + i for i in range(8)], partner_rank=1)
    return m


def _fm(v, n):
    return np.ascontiguousarray(np.asarray(v, np.float32).reshape(n, 128).T)


def _prep(inp):
    f = lambda k: np.asarray(inp[k], np.float32)
    w_in_a = f("w_in_a")[0]
    sw32 = np.r_[16:32, 0:16]
    wa = np.concatenate([w_in_a[:, 0:512], w_in_a[:, 512:544], w_in_a[:, 512:544][:, sw32], w_in_a[:, 544:2080]], axis=1)
    uq = f("mla_w_uq")[0].reshape(256, 8, 96)
    uq_sw = uq.copy()
    uq_sw[:, :, 64:96] = uq[:, :, 64:96][:, :, sw32]
    wuq = np.concatenate([uq.reshape(256, 768), uq_sw.reshape(256, 768)], axis=1)
    ukv = f("mla_w_ukv")[0].reshape(256, 8, 128)
    wukv = np.concatenate([ukv[:, :, 0:64].reshape(256, 512), ukv[:, :, 64:128].reshape(256, 512)], axis=1)
    w_in_c = f("w_in_c")[0]
    sw128 = np.r_[64:128, 0:64]
    q = w_in_c[:, 0:1024].reshape(D, 8, 128)
    k = w_in_c[:, 1024:1280].reshape(D, 2, 128)
    wc = np.concatenate([w_in_c, q[:, :, sw128].reshape(D, 1024), k[:, :, sw128].reshape(D, 256)], axis=1)
    shared = dict(
        wmod=f("w_mod"), wa=np.ascontiguousarray(wa), wuq=np.ascontiguousarray(wuq), wukv=np.ascontiguousarray(wukv),
        woa=f("w_out_a")[0], wc=np.ascontiguousarray(wc), woc=f("w_out_c")[0], wfi=f("w_ffn_in"), wfo=f("w_ffn_out"),
    )
    cstm = np.zeros((128, 640), np.float32)
    for m_ in range(128):
        cstm[64 * (m_ // 64) + 63 - (m_ % 64), 512 + m_] = 1.0
    cstm[:, 0:128] = np.eye(128, dtype=np.float32)
    cstm[64, 128:256] = 1.0
    cstm[0, 256:384] = 1.0
    cstm[:, 384:512] = 1.0
    shared["cst"] = cstm
    vecs = np.zeros((128, NVEC), np.float32)

    def put(nm, arr):
        o, n = VMAP[nm]
        vecs[:, o:o + n] = arr
    bm = f("b_mod")
    for l in range(2):
        put(f"bmod{l}", _fm(bm[l], 48))
        put(f"gmix{l}", _fm(f("norm_mix")[l], 8))
        put(f"gffn{l}", _fm(f("norm_ffn")[l], 8))
    put("gfin", _fm(f("norm_final"), 8))
    put("gq", _fm(f("mla_q_norm")[0], 2))
    put("gkv", _fm(f("mla_kv_norm")[0], 2))
    gq, gk = f("gqa_q_norm")[0], f("gqa_k_norm")[0]
    put("eps", np.full((128, 1), EPS, np.float32))
    put("ggq", gq[:, None]); put("ggqs", gq[sw128][:, None]); put("ggk", gk[:, None]); put("ggks", gk[sw128][:, None])
    rpb = f("na_rpb")[0]
    xs_all, xp_all = f("x_sample"), f("x_prompt")
    c_all, c_ctx = f("c"), f("c_ctx")
    in_maps, perms = [], []
    for core in range(8):
        bsz, half = core // 2, core % 2
        lrows = np.arange(32)
        grows = lrows if half == 0 else 63 - lrows
        perm = (grows[:, None] * 64 + np.arange(64)[None, :]).reshape(-1)
        perms.append(perm)
        d = dict(shared)
        d["xs"] = np.ascontiguousarray(xs_all[bsz][perm])
        d["xp"] = np.ascontiguousarray(xp_all[2 * core:2 * core + 2].reshape(TP, D))
        v = vecs.copy()
        o, n = VMAP["cond"]
        cd = np.stack([_fm(c_all[bsz], 8), _fm(c_ctx, 8)], axis=2).reshape(128, 16)
        v[:, o:o + n] = cd
        d["vec"] = v
        d["c_ckv"] = f("cache_mla_ckv")[bsz, 0]
        d["c_kr"] = f("cache_mla_krope")[bsz, 0]
        d["c_nk"] = np.ascontiguousarray(f("cache_na_k")[bsz, 0].reshape(256, 512))
        d["c_nv"] = np.ascontiguousarray(f("cache_na_v")[bsz, 0].reshape(256, 512))
        d["c_gk"] = np.ascontiguousarray(f("cache_gqa_k")[bsz, 0].reshape(256, 256))
        d["c_gv"] = np.ascontiguousarray(f("cache_gqa_v")[bsz, 0].reshape(256, 256))
        trow = np.repeat(grows, 64)
        tcol = np.tile(np.arange(64), 32)
        d["ropem"] = _rope_tables(trow, tcol, 32)
        d["ropeg"] = _rope_tables(trow, tcol, 128)
        rp = np.zeros((8, R2, RW), np.float32)
        r_use = rpb if half == 0 else rpb[:, ::-1, :]
        for dr in range(15):
            rp[:, 1 + (RR - 1) - (dr + 4), 64:95] = r_use[:, dr, ::-1]
        d["rpbp"] = rp
        d["masks"] = _na_masks(half)
        in_maps.append(d)
    return in_maps, perms


_NC_CACHE = {}


def kernel(**inputs):
    in_maps, perms = _prep(inputs)
    if "nc" not in _NC_CACHE:
        _NC_CACHE["nc"] = build()
    nc = _NC_CACHE["nc"]
    res = run_bass_kernel_spmd(nc, in_maps, core_ids=list(range(8)))
    R = res.results
    y_prompt = np.zeros((16, 256, D), np.float32)
    y_sample = np.zeros((4, 4096, D), np.float32)
    s_ckv = np.zeros((16, 1, 256, 256), np.float32)
    s_kr = np.zeros((16, 1, 256, 32), np.float32)
    s_nk = np.zeros((16, 1, 256, 8, 64), np.float32)
    s_nv = np.zeros((16, 1, 256, 8, 64), np.float32)
    s_gk = np.zeros((16, 1, 256, 2, 128), np.float32)
    s_gv = np.zeros((16, 1, 256, 2, 128), np.float32)
    for core in range(8):
        r = R[core]
        y_sample[core // 2][perms[core]] = r["y_s"]
        sl = slice(2 * core, 2 * core + 2)
        y_prompt[sl] = r["y_p"].reshape(2, 256, D)
        s_ckv[sl, 0] = r["st_ckv"].reshape(2, 256, 256)
        s_kr[sl, 0] = r["st_kr"].reshape(2, 256, 32)
        s_nk[sl, 0] = r["st_nk"].reshape(2, 256, 8, 64)
        s_nv[sl, 0] = r["st_nv"].reshape(2, 256, 8, 64)
        s_gk[sl, 0] = r["st_gk"].reshape(2, 256, 2, 128)
        s_gv[sl, 0] = r["st_gv"].reshape(2, 256, 2, 128)
    return (y_prompt, y_sample, s_ckv, s_kr, s_nk, s_nv, s_gk, s_gv)
```
